# Optimizing a Trainium2 kernel written in Bass

```python
import math
import jax
import jax.numpy as jnp
from jax import lax
import numpy as np

D_MODEL = 1024
BATCH = 4
SEQ = 4096
DEPTH = 1

ATT_HEADS = 4
ATT_HEAD_DIM = 64
ATT_WIDTH = ATT_HEADS * 2 * ATT_HEAD_DIM
ROPE_DIM = ATT_HEAD_DIM // 4
ROPE_THETA = 500000.0
Q_BLOCK = 128

SSM_INNER = 512
SSM_HEAD_DIM = 64
SSM_HEADS = SSM_INNER // SSM_HEAD_DIM
SSM_GROUPS = 2
SSM_HEADS_PER_GROUP = SSM_HEADS // SSM_GROUPS
SSM_STATE = 128
SSM_CONV = 3
SSM_CHUNK = 128
CONV_CH = SSM_INNER + 2 * SSM_GROUPS * SSM_STATE

MIX_WIDTH = ATT_WIDTH + SSM_INNER
IN_SPLITS = (ATT_WIDTH, 2 * ATT_WIDTH, 3 * ATT_WIDTH, 3 * ATT_WIDTH + SSM_INNER,
             3 * ATT_WIDTH + SSM_INNER + CONV_CH)
IN_COLS = 3 * ATT_WIDTH + SSM_INNER + CONV_CH + SSM_HEADS

D_FF = 2816
FFN_CONV = 3

N_MOD = 6
EPS = 1e-6

kernel_name = 'hybrid_diffattn_ssd_encoder_layer'


def rmsnorm(x, w):
    xf = x.astype(jnp.float32)
    y = xf * lax.rsqrt(jnp.mean(xf * xf, axis=-1, keepdims=True) + EPS)
    return (y * w.astype(jnp.float32)).astype(x.dtype)


def dwconv_centred(x, w, b):
    k = w.shape[0]
    pad = k // 2
    y = lax.conv_general_dilated(
        x, w.astype(x.dtype)[:, None, :], window_strides=(1,), padding=[(pad, pad)],
        dimension_numbers=('NWC', 'WIO', 'NWC'), feature_group_count=x.shape[-1])
    return y + b.astype(x.dtype)


def partial_rope(t, cos, sin):
    half = ROPE_DIM // 2
    t1 = t[..., :half]
    t2 = t[..., half:ROPE_DIM]
    return jnp.concatenate([t1 * cos - t2 * sin, t2 * cos + t1 * sin, t[..., ROPE_DIM:]], axis=-1)


def diff_attention(q, k, v, lam, lam_init, subln_w):
    b, s = q.shape[0], q.shape[1]
    scale = ATT_HEAD_DIM ** -0.5
    nb = s // Q_BLOCK
    qb = q.reshape(b, nb, Q_BLOCK, 2 * ATT_HEADS, ATT_HEAD_DIM).transpose(1, 0, 2, 3, 4)

    def block(q_blk):
        sc = jnp.einsum('bqhd,bkhd->bhqk', q_blk, k).astype(jnp.float32) * scale
        p = jax.nn.softmax(sc, axis=-1).reshape(b, ATT_HEADS, 2, Q_BLOCK, s)
        a = p[:, :, 0] - lam * p[:, :, 1]
        return jnp.einsum('bhqk,bkhe->bqhe', a.astype(v.dtype), v)

    o = lax.map(block, qb)
    o = o.transpose(1, 0, 2, 3, 4).reshape(b, s, ATT_HEADS, 2 * ATT_HEAD_DIM)
    o = rmsnorm(o, subln_w) * (1.0 - lam_init)
    return o.reshape(b, s, ATT_WIDTH)


def segsum(a):
    t = a.shape[-1]
    cs = jnp.cumsum(a, axis=-1)
    seg = cs[..., :, None] - cs[..., None, :]
    mask = jnp.tril(jnp.ones((t, t), dtype=bool))
    return jnp.where(mask, seg, -jnp.inf)


def ssd_chunked(xdt, adt, bm, cm):
    b, s, g, r, p = xdt.shape
    n = bm.shape[-1]
    nc = s // SSM_CHUNK
    x = xdt.reshape(b, nc, SSM_CHUNK, g, r, p)
    a = adt.reshape(b, nc, SSM_CHUNK, g, r).transpose(0, 3, 4, 1, 2)
    bq = bm.reshape(b, nc, SSM_CHUNK, g, n)
    cq = cm.reshape(b, nc, SSM_CHUNK, g, n)
    a_cs = jnp.cumsum(a, axis=-1)
    decay_in = jnp.exp(segsum(a))
    cb = jnp.einsum('bclgn,bcsgn->bgcls', cq, bq)
    y_diag = jnp.einsum('bgcls,bgrcls,bcsgrp->bclgrp', cb, decay_in, x)
    decay_st = jnp.exp(a_cs[..., -1:] - a_cs)
    states = jnp.einsum('bcsgn,bgrcs,bcsgrp->bcgrpn', bq, decay_st, x)
    chunk_decay = jnp.exp(a_cs[..., -1])

    def step(h, inp):
        st, dcy = inp
        return dcy[..., None, None] * h + st, h

    h0 = jnp.zeros((b, g, r, p, n), dtype=states.dtype)
    _, prev = lax.scan(step, h0, (states.transpose(1, 0, 2, 3, 4, 5),
                                  chunk_decay.transpose(3, 0, 1, 2)))
    prev = prev.transpose(1, 0, 2, 3, 4, 5)
    y_off = jnp.einsum('bclgn,bcgrpn,bgrcl->bclgrp', cq, prev, jnp.exp(a_cs))
    return (y_diag + y_off).reshape(b, s, g * r * p)


def ssd_mixer(z, xbc, dt_raw, conv_w, conv_b, dt_bias, a_log, d_skip, norm_w):
    b, s = z.shape[0], z.shape[1]
    g, r, p, n = SSM_GROUPS, SSM_HEADS_PER_GROUP, SSM_HEAD_DIM, SSM_STATE
    xbc = jax.nn.silu(dwconv_centred(xbc, conv_w, conv_b))
    xs = xbc[..., :SSM_INNER].reshape(b, s, g, r, p)
    bm = xbc[..., SSM_INNER:SSM_INNER + g * n].reshape(b, s, g, n)
    cm = xbc[..., SSM_INNER + g * n:].reshape(b, s, g, n)
    dt_f = jax.nn.softplus(dt_raw.astype(jnp.float32) + dt_bias[0].astype(jnp.float32)).reshape(b, s, g, r)
    dt_b = jax.nn.softplus(dt_raw.astype(jnp.float32) + dt_bias[1].astype(jnp.float32)).reshape(b, s, g, r)
    a_f = -jnp.exp(a_log[0].astype(jnp.float32)).reshape(g, r)
    a_b = -jnp.exp(a_log[1].astype(jnp.float32)).reshape(g, r)
    y_f = ssd_chunked(xs * dt_f[..., None], dt_f * a_f, bm, cm)
    y_b = jnp.flip(ssd_chunked(jnp.flip(xs * dt_b[..., None], 1), jnp.flip(dt_b * a_b, 1),
                               jnp.flip(bm, 1), jnp.flip(cm, 1)), 1)
    skip = xs * (d_skip[0] + d_skip[1]).reshape(g, r)[..., None]
    y = y_f + y_b + skip.reshape(b, s, SSM_INNER)
    gated = (y * jax.nn.silu(z.astype(jnp.float32))).reshape(b, s, g, SSM_INNER // g)
    out = rmsnorm(gated, norm_w.reshape(g, SSM_INNER // g))
    return out.reshape(b, s, SSM_INNER).astype(z.dtype)


def conv_glu_ffn(h, w_in, conv_w, conv_b, w_out):
    gu = h @ w_in
    gate, up = jnp.split(gu, 2, axis=-1)
    gate = dwconv_centred(gate, conv_w, conv_b)
    return (jax.nn.silu(gate) * up) @ w_out


def setup_inputs(seed: int = 0) -> dict:
    key = jax.random.key(seed)
    ks = jax.random.split(key, 25)
    f32 = jnp.float32
    L, D = DEPTH, D_MODEL

    def nrm(k, shape, scale):
        return jax.random.normal(k, shape, f32) * scale

    def gain(k, shape):
        return 1.0 + 0.02 * jax.random.normal(k, shape, f32)

    dt0 = jnp.exp(jax.random.uniform(ks[14], (L, 2, SSM_HEADS), f32, math.log(1e-3), math.log(1e-1)))
    dt0 = jnp.maximum(dt0, 1e-4)
    dt_bias = dt0 + jnp.log(-jnp.expm1(-dt0))
    a_log = jnp.log(jax.random.uniform(ks[15], (L, 2, SSM_HEADS), f32, 1.0, 16.0))
    positions = jnp.broadcast_to(jnp.arange(SEQ, dtype=jnp.int32)[None, :], (BATCH, SEQ))
    return {
        'x': nrm(ks[0], (BATCH, SEQ, D), 1.0),
        'c': nrm(ks[1], (BATCH, D), 1.0),
        'positions': positions,
        'w_ada': nrm(ks[2], (L, D, N_MOD * D), 0.5 * D ** -0.5),
        'b_ada': nrm(ks[3], (L, N_MOD * D), 0.02),
        'norm1_w': gain(ks[4], (L, D)),
        'w_in': nrm(ks[5], (L, D, IN_COLS), D ** -0.5),
        'lambda_q1': nrm(ks[6], (L, ATT_HEAD_DIM), 0.1),
        'lambda_k1': nrm(ks[7], (L, ATT_HEAD_DIM), 0.1),
        'lambda_q2': nrm(ks[8], (L, ATT_HEAD_DIM), 0.1),
        'lambda_k2': nrm(ks[9], (L, ATT_HEAD_DIM), 0.1),
        'subln_w': gain(ks[10], (L, 2 * ATT_HEAD_DIM)),
        'conv_w': nrm(ks[11], (L, SSM_CONV, CONV_CH), SSM_CONV ** -0.5),
        'conv_b': nrm(ks[12], (L, CONV_CH), 0.02),
        'dt_bias': dt_bias,
        'a_log': a_log,
        'd_skip': 1.0 + nrm(ks[13], (L, 2, SSM_HEADS), 0.1),
        'ssm_norm_w': gain(ks[16], (L, SSM_INNER)),
        'w_out': nrm(ks[17], (L, MIX_WIDTH, D), MIX_WIDTH ** -0.5),
        'norm2_w': gain(ks[18], (L, D)),
        'w_ffn_in': nrm(ks[19], (L, D, 2 * D_FF), D ** -0.5),
        'ffn_conv_w': nrm(ks[20], (L, FFN_CONV, D_FF), FFN_CONV ** -0.5),
        'ffn_conv_b': nrm(ks[21], (L, D_FF), 0.02),
        'w_ffn_out': nrm(ks[22], (L, D_FF, D), D_FF ** -0.5),
        'final_norm_w': gain(ks[23], (D,)),
    }


def reference(x, c, positions, w_ada, b_ada, norm1_w, w_in, lambda_q1, lambda_k1, lambda_q2,
              lambda_k2, subln_w, conv_w, conv_b, dt_bias, a_log, d_skip, ssm_norm_w, w_out,
              norm2_w, w_ffn_in, ffn_conv_w, ffn_conv_b, w_ffn_out, final_norm_w):
    b, s, _ = x.shape
    inv_freq = ROPE_THETA ** (-jnp.arange(0, ROPE_DIM, 2, dtype=jnp.float32) / ROPE_DIM)
    ang = positions.astype(jnp.float32)[..., None] * inv_freq
    cos = jnp.cos(ang)[:, :, None, :].astype(x.dtype)
    sin = jnp.sin(ang)[:, :, None, :].astype(x.dtype)
    c_act = jax.nn.silu(c)
    for l in range(DEPTH):
        lam_init = 0.8 - 0.6 * math.exp(-0.3 * l)
        mod = c_act @ w_ada[l] + b_ada[l]
        sh1, sc1, g1, sh2, sc2, g2 = [m[:, None, :] for m in jnp.split(mod, N_MOD, axis=-1)]

        h = rmsnorm(x, norm1_w[l]) * (1.0 + sc1) + sh1
        proj = h @ w_in[l]
        q, k, v, z, xbc, dt_raw = jnp.split(proj, IN_SPLITS, axis=-1)
        q = partial_rope(q.reshape(b, s, 2 * ATT_HEADS, ATT_HEAD_DIM), cos, sin)
        k = partial_rope(k.reshape(b, s, 2 * ATT_HEADS, ATT_HEAD_DIM), cos, sin)
        v = v.reshape(b, s, ATT_HEADS, 2 * ATT_HEAD_DIM)
        lam = (jnp.exp(jnp.sum(lambda_q1[l].astype(jnp.float32) * lambda_k1[l].astype(jnp.float32)))
               - jnp.exp(jnp.sum(lambda_q2[l].astype(jnp.float32) * lambda_k2[l].astype(jnp.float32)))
               + lam_init)
        att = diff_attention(q, k, v, lam, lam_init, subln_w[l])
        ssm = ssd_mixer(z, xbc, dt_raw, conv_w[l], conv_b[l], dt_bias[l], a_log[l],
                        d_skip[l], ssm_norm_w[l])
        mix = jnp.concatenate([att, ssm], axis=-1) @ w_out[l]
        x = x + g1 * mix

        h = rmsnorm(x, norm2_w[l]) * (1.0 + sc2) + sh2
        x = x + g2 * conv_glu_ffn(h, w_ffn_in[l], ffn_conv_w[l], ffn_conv_b[l], w_ffn_out[l])
    return rmsnorm(x, final_norm_w)
```

```python
import numpy as np
import concourse.bass as bass
import concourse.mybir as mybir
from concourse.bass_utils import run_bass_kernel_spmd

F32 = mybir.dt.float32; BF16 = mybir.dt.bfloat16; I32 = mybir.dt.int32
AF = mybir.ActivationFunctionType; ALU = mybir.AluOpType; AX = mybir.AxisListType
DTSIZE = {F32: 4, BF16: 2, I32: 4}
EPS = 1e-6
T = 4096; D = 1024; NOWN = 2048; NQ = 2176; DFF = 2816; NFF = 22
N1W, N2W, CW, CB, FCW, FCB, CC, IFQ, NC = 0, 8, 16, 40, 48, 114, 136, 144, 148
SUBLN, SSMNW, FNW, DTB, ALOG, DSK, LQ1, LK1, LQ2, LK2, NR = 0, 128, 640, 1664, 1680, 1696, 1712, 1776, 1840, 1904, 1968
MAGIC = float(1.5 * 2 ** 23); C1 = 6.28125; C2 = float(2 * np.pi - 6.28125)


class Buf:
    __slots__ = ("name", "lw", "rd", "ds", "excl")

    def __init__(self, name, excl=False):
        self.name = name; self.lw = None; self.rd = {}; self.ds = None
        self.excl = excl


class K:
    SB_BASE = 16512; SB_END = 229344

    def __init__(self):
        nc = self.nc = bass.Bass("TRN2", target_bir_lowering=False)
        self.eng = {"pe": nc.tensor, "act": nc.scalar, "dve": nc.vector, "pool": nc.gpsimd, "sp": nc.sync}
        self.sem = {}; self.cnt = {}
        for e in ("pe", "act", "dve", "pool"):
            self.sem[e] = nc.semaphore("s_" + e).__enter__(); self.cnt[e] = 0
        self.seen = {e: {} for e in self.eng}; self.pend = {}
        self.sb_ptr = self.SB_BASE; self.sb_end = self.SB_END; self.uid = 0; self.peak = 0

    def sb(self, name, shape, dt):
        per = int(np.prod(shape[1:])) * DTSIZE[dt]; per = (per + 31) // 32 * 32
        off = self.sb_ptr; self.sb_ptr += per
        assert self.sb_ptr <= self.sb_end, (name, self.sb_ptr - self.sb_end)
        self.peak = max(self.peak, self.sb_ptr)
        self.uid += 1
        return self.nc.alloc_sbuf_tensor_at(f"{name}_{self.uid}", list(shape), dt, offset=off)

    def sb_top(self, name, shape, dt):
        per = int(np.prod(shape[1:])) * DTSIZE[dt]; per = (per + 31) // 32 * 32
        self.sb_end -= per
        assert self.sb_ptr <= self.sb_end, (name, self.sb_ptr - self.sb_end)
        self.uid += 1
        return self.nc.alloc_sbuf_tensor_at(f"{name}_{self.uid}", list(shape), dt, offset=self.sb_end)

    def mark(self): return self.sb_ptr

    def release(self, m): self.sb_ptr = m

    def dsem(self, name):
        s = self.nc.semaphore("d_" + name).__enter__(); self.sem[name] = s; self.cnt[name] = 0; return name

    def _waits(self, e, reads, writes):
        need = {}

        def add(t):
            if t is None: return
            s, v = t
            if s == "pe" and e == "pe": return
            if need.get(s, 0) < v: need[s] = v
        for b in reads:
            add(b.lw)
            if b.excl:
                for s, t in b.rd.items():
                    if s != e: add(t)
        for b in writes:
            add(b.lw)
            for s, t in b.rd.items():
                add(t)
        sn = self.seen[e]
        for s, v in need.items():
            if sn.get(s, 0) < v:
                self.eng[e].wait_ge(self.sem[s], v); sn[s] = v

    def _done(self, ticket, reads, writes, eng):
        for b in writes: b.lw = ticket; b.rd = {}
        for b in reads: b.rd[eng] = ticket

    def op(self, e, ins_fn, reads=(), writes=(), inc=True):
        self._waits(e, reads, writes)
        ins = ins_fn(self.eng[e])
        pr_, pw_ = self.pend.setdefault(e, ([], []))
        if inc:
            self.cnt[e] += 1; ins.then_inc(self.sem[e], 1)
            self._done((e, self.cnt[e]), list(reads) + pr_, list(writes) + pw_, e)
            self.pend[e] = ([], [])
        else:
            for b in reads:
                if not any(b is x for x in pr_): pr_.append(b)
            for b in writes:
                if not any(b is x for x in pw_): pw_.append(b)
        return ins

    def dma(self, q, ds, out, in_, reads=(), writes=()):
        prim = writes[0] if len(writes) else reads[0]
        if prim.ds is None:
            prim.ds = self.dsem(f"b{len(self.sem)}")
        ds = prim.ds
        self._waits(q, reads, writes)
        ins = self.eng[q].dma_start(out=out, in_=in_)
        self.cnt[ds] += 16; ins.then_inc(self.sem[ds], 16)
        self._done((ds, self.cnt[ds]), reads, writes, "dma_" + ds)
        return ins

    def barrier(self, skip=()):
        tot = {s: c for s, c in self.cnt.items() if c > 0 and s not in skip}
        for e in self.eng:
            sn = self.seen[e]
            for s, v in tot.items():
                if s == e: continue
                if sn.get(s, 0) < v:
                    self.eng[e].wait_ge(self.sem[s], v); sn[s] = v


def build(debug=False, upto="all"):
    k = K(); nc = k.nc

    def din(name, shape, dt=F32): return nc.dram_tensor(name, list(shape), dt, kind="ExternalInput").ap()

    def dout(name, shape, dt=F32): return nc.dram_tensor(name, list(shape), dt, kind="ExternalOutput").ap()
    x_d = din("x", [T, D]); pos_d = din("pos", [1, T], I32)
    wada_d = din("w_ada", [D, 6 * D]); bada_d = din("b_ada", [1, 6 * D])
    win_d = din("w_in", [D, 3080]); wout_d = din("w_out", [D, D])
    wfi_d = din("w_ffn_in", [D, 2 * DFF]); wfo_d = din("w_ffn_out", [DFF, D])
    cmat_d = din("cmat", [128, 7, 128]); pcol_d = din("pcol", [128, NC]); prow_d = din("prow", [1, NR])
    out_d = dout("out", [NOWN, D])
    x1s_d = nc.dram_tensor("x1s", [NOWN, D], F32).ap()
    hTs_d = nc.dram_tensor("hTs", [8, 128, 8 * 512], BF16).ap()
    dbg = {}
    if debug:
        dbg["mod"] = dout("dbg_mod", [1, 6 * D])
        dbg["ssm"] = dout("dbg_ssm", [512, NQ], BF16)
        dbg["att"] = dout("dbg_att", [512, NQ], BF16)
        dbg["x1"] = dout("dbg_x1", [NOWN, D])
    wv = win_d.rearrange("(kc p) n -> p kc n", p=128)
    ld = k.dsem("ld"); lw = k.dsem("lw"); st = k.dsem("st")

    def act(out, in_, func, reads, writes, **kw):
        return k.op("act", lambda e: e.activation(out=out, in_=in_, func=func, **kw), reads, writes)

    def tt(eng, out, in0, in1, op, reads, writes):
        return k.op(eng, lambda e: e.tensor_tensor(out=out, in0=in0, in1=in1, op=op), reads, writes)

    def ts(eng, out, in0, s1, s2, op0, op1, reads, writes):
        if s2 is None:
            return k.op(eng, lambda e: e.tensor_scalar(out=out, in0=in0, scalar1=s1, scalar2=None, op0=op0), reads, writes)
        return k.op(eng, lambda e: e.tensor_scalar(out=out, in0=in0, scalar1=s1, scalar2=s2, op0=op0, op1=op1), reads, writes)

    def stt(eng, out, in0, scalar, in1, op0, op1, reads, writes):
        return k.op(eng, lambda e: e.scalar_tensor_tensor(out=out, in0=in0, scalar=scalar, in1=in1, op0=op0, op1=op1), reads, writes)

    def cp(eng, out, in_, reads, writes):
        if eng == "act":
            return act(out, in_, AF.Copy, reads, writes)
        return k.op(eng, lambda e: e.tensor_copy(out=out, in_=in_), reads, writes)

    def mm(out, lhsT, rhs, start, stop, reads, writes, inc):
        return k.op("pe", lambda e: e.matmul(out, lhsT=lhsT, rhs=rhs, start=start, stop=stop), reads, writes, inc=inc)

    def trp(out, in_, ident, reads, writes, inc):
        return k.op("pe", lambda e: e.transpose(out=out, in_=in_, identity=ident), reads, writes, inc=inc)

    def bc3(ap, a, b):
        return ap.unsqueeze(2).to_broadcast([128, a, b])

    pb = [nc.alloc_psum_tensor(f"pb{i}", [128, 512], F32) for i in range(8)]
    bpb = [Buf(f"pb{i}", excl=True) for i in range(8)]

    def pbf(i):
        return pb[i][:].bitcast(BF16)

    cm = k.sb("cm", [128, 7, 128], F32); bcm = Buf("cm")
    pc = k.sb("pc", [128, NC], F32); bpc = Buf("pc")
    pr = k.sb("pr", [128, NR], F32); bpr = Buf("pr")
    k.dma("sp", ld, cm[:], cmat_d, writes=[bcm])
    k.dma("sp", ld, pc[:], pcol_d, writes=[bpc])
    k.dma("sp", ld, pr[:], prow_d.to_broadcast([128, NR]), writes=[bpr])
    cmb = k.sb("cmb", [128, 4, 128], BF16); bcmb = Buf("cmb")
    for j, src in enumerate((0, 1, 3, 6)):
        cp("dve", cmb[:, j, :], cm[:, src, :], [bcm], [bcmb])
    identb = cmb[:, 0, :]; TUb = cmb[:, 1, :]; TLb = cmb[:, 2, :]; Rmb = cmb[:, 3, :]
    TU = cm[:, 1, :]; SL = cm[:, 2, :]; TL = cm[:, 3, :]; SU = cm[:, 4, :]; ONES = cm[:, 5, :]
    posi = k.sb("posi", [128, 512], I32); bposi = Buf("posi")
    sv = k.sb("sv", [128, 128], F32); bsv = Buf("sv")
    g1b = k.sb("g1b", [128, D], F32); g2b = k.sb("g2b", [128, D], F32); bgb = Buf("gb")
    srow = k.sb("srow", [128, D], F32); shrow = k.sb("shrow", [128, D], F32); bsrow = Buf("srow")
    dg = k.sb("dg", [128, 128], F32); bdg = Buf("dg")

    def make_rows(sc, shc):
        for which, (col, dst) in enumerate(((sc, srow), (shc, shrow))):
            for hf in range(2):
                for q in range(4):
                    kc = hf * 4 + q
                    ts("dve", dg[:], cm[:, 0, :], col[:, kc:kc + 1], None, ALU.mult, None, [bcm, bsv], [bdg])
                    mm(pb[3][:, q * 128:(q + 1) * 128], ONES, dg[:], True, True, [bcm, bdg], [bpb[3]], True)
                cp("dve", dst[:, hf * 512:(hf + 1) * 512], pb[3][:, :], [bpb[3]], [bsrow])

    modT = k.sb("modT", [128, 48], F32); bmodT = Buf("modT")
    wav = wada_d.rearrange("(kc p) n -> p kc n", p=128)
    NHB = 48
    NWAB = 4
    ada_state = {"dma": 0, "mm": 0}
    AD = {}

    def ada_alloc(bank, bank2, nwab=4, nbrow=4, nrow=2):
        AD["top"] = k.sb_end
        AD["cact"] = k.sb_top("cact", [128, 8], BF16); AD["bcact"] = Buf("cact")
        act(AD["cact"][:], pc[:, CC:CC + 8], AF.Silu, [bpc], [AD["bcact"]])
        AD["brow"] = [k.sb_top(f"brow{i}", [1, 512], F32) for i in range(nbrow)]; AD["bbrow"] = [Buf(f"brow{i}") for i in range(nbrow)]
        AD["row"] = [k.sb_top(f"arow{i}", [1, 512], F32) for i in range(nrow)]; AD["brw"] = [Buf(f"arow{i}") for i in range(nrow)]
        AD["nbrow"] = nbrow; AD["nrow"] = nrow
        AD["wab"] = [k.sb_top(f"wab{i}", [128, 8, 128], BF16) for i in range(nwab)]; AD["bwab"] = [Buf(f"wab{i}") for i in range(nwab)]
        AD["bank"] = bank; AD["bank2"] = bank2; AD["nwab"] = nwab
        print("ada_alloc free bytes:", k.sb_end - k.sb_ptr)

    def ada_release():
        k.sb_end = AD["top"]

    def ada_dma():
        i = ada_state["dma"]
        if i >= NHB: return
        ada_state["dma"] += 1
        j = i % AD["nwab"]
        if i % 4 == 0:
            cg = i // 4
            nb = AD["nbrow"]
            k.dma("sp", ld, AD["brow"][cg % nb][:], bada_d[:, cg * 512:(cg + 1) * 512], writes=[AD["bbrow"][cg % nb]])
        k.dma("pool", lw, AD["wab"][j][:], wav[:, :, i * 128:(i + 1) * 128], writes=[AD["bwab"][j]])

    def ada_half():
        i = ada_state["mm"]
        if i >= NHB: return
        ada_state["mm"] += 1
        while ada_state["dma"] <= i:
            ada_dma()
        j = i % AD["nwab"]; cg = i // 4; hh = i % 4
        bk = AD["bank"]
        for kc in range(8):
            mm(pb[bk][0:1, hh * 128:(hh + 1) * 128], AD["cact"][:, kc:kc + 1], AD["wab"][j][:, kc, :], kc == 0, kc == 7,
               [AD["bcact"], AD["bwab"][j]], [bpb[bk]], kc == 7)
        if ada_state["dma"] < min(NHB, AD["limit"]):
            ada_dma()
        row = AD["row"][cg % AD["nrow"]]; brw = AD["brw"][cg % AD["nrow"]]
        nb = AD["nbrow"]
        qs = slice(hh * 128, (hh + 1) * 128)
        tt("dve", row[0:1, qs], pb[bk][0:1, qs], AD["brow"][cg % nb][0:1, qs], ALU.add, [bpb[bk], AD["bbrow"][cg % nb]], [brw])
        if hh == 3:
            if debug:
                k.dma("pool", st, dbg["mod"][:, cg * 512:(cg + 1) * 512], row[0:1, :], reads=[brw])
            b2 = AD["bank2"]
            for q in range(4):
                mm(pb[b2][:, q:q + 1], row[0:1, q * 128:(q + 1) * 128], cm[0:1, 5, 0:1], True, True, [brw, bcm], [bpb[b2]], q == 3)
            cp("dve", modT[:, cg * 4:cg * 4 + 4], pb[b2][:, 0:4], [bpb[b2]], [bmodT])
            for gt, c0 in ((g1b, 2 * D), (g2b, 5 * D)):
                if c0 <= cg * 512 < c0 + D:
                    mm(pb[b2][:, :], cm[0:1, 5, :], row[0:1, :], True, True, [brw, bcm], [bpb[b2]], True)
                    cp("dve", gt[:, cg * 512 - c0:cg * 512 - c0 + 512], pb[b2][:, :], [bpb[b2]], [bgb])

    def ada_block():
        for _ in range(4):
            ada_half()

    ada_alloc(5, 2)
    AD["limit"] = 16
    for _ in range(NWAB):
        ada_dma()
    for cg in range(4):
        ada_block()
    ts("dve", sv[:, 64:72], modT[:, 8:16], 1.0, None, ALU.add, None, [bmodT], [bsv])
    tt("dve", sv[:, 0:8], sv[:, 64:72], pc[:, N1W:N1W + 8], ALU.mult, [bsv, bpc], [bsv])
    cp("dve", sv[:, 8:16], modT[:, 0:8], [bmodT], [bsv])

    def ada_finish():
        while ada_state["mm"] < NHB:
            ada_half()
        ts("dve", sv[:, 72:80], modT[:, 32:40], 1.0, None, ALU.add, None, [bmodT, bsv], [bsv])
        tt("dve", sv[:, 16:24], sv[:, 72:80], pc[:, N2W:N2W + 8], ALU.mult, [bsv, bpc], [bsv])
        cp("dve", sv[:, 24:32], modT[:, 24:32], [bmodT], [bsv])
    tt("dve", sv[:, 64:128], pr[:, LQ1:LQ1 + 64], pr[:, LK1:LK1 + 64], ALU.mult, [bpr, bsv], [bsv])
    k.op("dve", lambda e: e.reduce_sum(out=sv[:, 33:34], in_=sv[:, 64:128], axis=AX.X), [bsv], [bsv])
    tt("dve", sv[:, 64:128], pr[:, LQ2:LQ2 + 64], pr[:, LK2:LK2 + 64], ALU.mult, [bpr, bsv], [bsv])
    k.op("dve", lambda e: e.reduce_sum(out=sv[:, 34:35], in_=sv[:, 64:128], axis=AX.X), [bsv], [bsv])
    act(sv[:, 33:35], sv[:, 33:35], AF.Exp, [bsv], [bsv])
    tt("dve", sv[:, 32:33], sv[:, 34:35], sv[:, 33:34], ALU.subtract, [bsv], [bsv])
    ts("dve", sv[:, 32:33], sv[:, 32:33], -0.2, None, ALU.add, None, [bsv], [bsv])
    nlam = sv[:, 32:33]
    act(sv[:, 40:56], pr[:, ALOG:ALOG + 16], AF.Exp, [bpr, bsv], [bsv])
    ts("dve", sv[:, 40:56], sv[:, 40:56], -1.0, None, ALU.mult, None, [bsv], [bsv])
    tt("dve", sv[:, 56:64], pr[:, DSK:DSK + 8], pr[:, DSK + 8:DSK + 16], ALU.add, [bpr, bsv], [bsv])
    sub8 = k.sb("sub8", [128, 128], F32); bsub8 = Buf("sub8")
    ts("dve", sub8[:], pr[:, SUBLN:SUBLN + 128], 0.8, None, ALU.mult, None, [bpr], [bsub8])
    s1c = sv[:, 0:8]; sh1c = sv[:, 8:16]; s2c = sv[:, 16:24]; sh2c = sv[:, 24:32]
    make_rows(s1c, sh1c)

    LN = {}
    bxn = [Buf(f"xn{i}") for i in range(2)]

    def ln_alloc():
        LN["xn"] = [k.sb(f"xn{i}", [128, D], BF16) for i in range(2)]
        LN["sqj"] = k.sb("sqj", [128, D], BF16)
    stt_ = [k.sb(f"stat{i}", [128, 8], F32) for i in range(2)]; bstat = [Buf(f"stat{i}") for i in range(2)]
    lnc = [0, 0]
    bsqj = Buf("sqj")
    TRB = (6, 7)

    epsc = k.sb("epsc", [128, 1], F32); bepsc = Buf("epsc")
    onec = k.sb("onec", [128, 1], F32)
    k.op("dve", lambda e: e.memset(epsc[:], EPS), [], [bepsc])
    k.op("dve", lambda e: e.memset(onec[:], 1.0), [], [bepsc])

    def rstd_from_ssq(stat, bst, n, inv_n):
        act(stat[:, 4:4 + n], stat[:, 0:n], AF.Ln, [bst, bepsc], [bst], bias=epsc[:, 0:1], scale=inv_n)
        act(stat[:, 4:4 + n], stat[:, 4:4 + n], AF.Exp, [bst], [bst], scale=-0.5)

    def ln_stage1(tiles):
        g = lnc[0] % 2; lnc[0] += 1
        stat = stt_[g]; bst = bstat[g]
        k.op("dve", lambda e: e.memset(stat[:], 0.0), [], [bst])
        tl = []
        for i, (xa, bx) in enumerate(tiles):
            act(LN["sqj"][:], xa, AF.Square, [bx], [bsqj, bst], accum_out=stat[:, i:i + 1])
            tl.append((xa, bx))
        rstd_from_ssq(stat, bst, len(tiles), 1.0 / D)
        return (tl, stat, bst)

    def ln_stage2(state, dst_fn, tmp=None):
        tl, stat, bst = state
        for i, (xa, bx) in enumerate(tl):
            j = lnc[1] % 2; lnc[1] += 1
            ta, bt = (xa, bx) if tmp is None else tmp
            stt("dve", ta, xa, stat[:, 4 + i:5 + i], srow[:], ALU.mult, ALU.mult, [bx, bst, bsrow], [bt])
            xn = LN["xn"]
            tt("dve", xn[j][:], ta, shrow[:], ALU.add, [bt, bsrow], [bxn[j]])
            bank = TRB[j]
            pv = pbf(bank)
            for kc in range(8):
                trp(pv[:, kc * 128:(kc + 1) * 128], xn[j][:, kc * 128:(kc + 1) * 128], identb, [bxn[j], bcmb], [bpb[bank]], kc == 7)
            d, bd = dst_fn(i, None)
            cp("act", d, pv[:, 0:1024].rearrange("p (c t) -> p c t", t=128), [bpb[bank]], [bd])

    def ln_group(tiles, dst_fn, sc, shc, tmp=None):
        ln_stage2(ln_stage1(tiles), dst_fn, tmp)

    bxtl = [Buf(f"xt{i}") for i in range(8)]
    xtc = [0]

    def alloc_xtl(nbuf=4):
        return [k.sb(f"xt{i}", [128, D], F32) for i in range(nbuf)]

    def load_x_tiles(xtl, tile_ids):
        res = []
        for t in tile_ids:
            i = xtc[0] % len(xtl); xtc[0] += 1
            k.dma("sp", ld, xtl[i][:], x_d[t * 128:(t + 1) * 128, :], writes=[bxtl[i]])
            res.append((xtl[i][:], bxtl[i]))
        return res

    def load_w_bf16(dst, bdst, src_ap, *unused, **unused_kw):
        k.dma("pool", lw, dst, src_ap, writes=[bdst])

    k.barrier()
    ada_release()
    mCore = k.mark()
    ssmT = k.sb("ssmT", [128, 4, NQ], BF16); bssmT = Buf("ssmT")
    Hb = k.sb("Hb", [128, 512], F32); bHb = Buf("Hb")
    k.op("dve", lambda e: e.memset(Hb[:], 0.0), [], [bHb])

    def ssd_phase(tile0, ntile, own, bg=None):
        n = ntile * 128
        nsc = 17 if own else ntile
        mS = k.mark()
        topS = k.sb_end
        ln_alloc()
        hT = k.sb_top("hT", [128, 8, n], BF16); bhT = [Buf(f"hT{i}") for i in range(ntile)]
        x_tok = k.sb("x_tok", [128, ntile, 512], BF16); bxtok = [Buf(f"xtok{i}") for i in range(ntile)]
        B_tok = k.sb("B_tok", [128, ntile, 256], BF16); bBtok = [Buf(f"Btok{i}") for i in range(ntile)]
        if own:
            BT = k.sb("BT", [128, 2, n], BF16); CT = k.sb("CT", [128, 2, n], BF16)
            bBT = [Buf("BT0"), Buf("BT1")]; bCT = [Buf("CT0"), Buf("CT1")]
            wz = k.sb_top("wz", [128, 8, 512], BF16); bwz = Buf("wz")
            sza = k.sb("sza", [128, 17, 512], BF16); bsza = [Buf(f"sza{i}") for i in range(17)]
        dta = k.sb("dta", [128, 2, ntile, 8], F32); bdta = Buf("dta")
        aa = k.sb("aa", [128, 2, ntile, 8], F32); baa = Buf("aa")
        EC = k.sb("EC", [128, 6, ntile, 8], F32); bEC = Buf("EC")
        wdt = k.sb("wdt", [128, 8, 8], BF16); bwdt = Buf("wdt")
        wcb = [k.sb(f"wcb{i}", [128, 8, 128], BF16) for i in range(2)]; bwcb = [Buf(f"wcb{i}") for i in range(2)]
        load_w_bf16(wcb[0][:], bwcb[0], wv[:, :, 2048:2048 + 128])
        load_w_bf16(wdt[:], bwdt, wv[:, :, 3072:3080])
        if own:
            load_w_bf16(wz[:], bwz, wv[:, :, 1536:2048])
        mA = k.mark()
        xtl = alloc_xtl(8)
        groups = [list(range(g0, min(ntile, g0 + 4))) for g0 in range(0, ntile, 4)]

        def hT_s1(ids):
            return ln_stage1(load_x_tiles(xtl, [tile0 + i for i in ids]))

        def hT_s2(ids, st_):
            ln_stage2(st_, lambda i, kc, ids=ids: (hT[:, :, ids[i] * 128:(ids[i] + 1) * 128], bhT[ids[i]]))
            g0 = ids[0]
            if own:
                while zq:
                    z_tile(zq.pop(0))
                zq.extend(ci for ci in ids if ci < 17)
            gq = (tile0 + g0) // 4
            if len(ids) == 4 and (own or gq >= 4) and not (own and gq >= 4):
                k.dma("pool", ld, hTs_d[gq].rearrange("p (c t) -> p c t", t=512), hT[:, :, g0 * 128:(g0 + 4) * 128],
                      reads=[bhT[i] for i in ids], writes=[bhts])
        zq = []

        def z_tile(ci):
            bk = 4 + (ci % 2)
            for kc in range(8):
                mm(pb[bk][:, :], hT[:, kc, ci * 128:(ci + 1) * 128], wz[:, kc, :], kc == 0, kc == 7, [bhT[ci], bwz], [bpb[bk]], kc == 7)
            act(sza[:, ci, :], pb[bk][:, :], AF.Silu, [bpb[bk]], [bsza[ci]])
        bhts = Buf("hTs_spill")
        st_next = hT_s1(groups[0])
        for gi, ids in enumerate(groups):
            st_cur = st_next
            if gi + 1 < len(groups):
                st_next = hT_s1(groups[gi + 1])
            hT_s2(ids, st_cur)
        while zq:
            z_tile(zq.pop(0))
        k.release(mA)
        k.barrier(skip=(bhts.ds,) if bhts.ds else ())
        raws = [k.sb(f"raw{i}", [128, n], F32) for i in range(2)]; braws = [Buf(f"raw{i}") for i in range(2)]
        acc = k.sb("acc", [128, n], F32); bacc = Buf("acc")
        cv = [k.sb("cv0", [128, n], BF16)] * 2; bcv = [Buf("cv0")] * 2
        wst = []; bwst = []
        wctr = [0]
        segs = [(c0, min(n, c0 + 512)) for c0 in range(0, n, 512)]
        pring = [0]
        nch = 8 if own else 6

        def xbc_mm(c):
            j = c % 2
            raw = raws[j]; braw = braws[j]
            if c > 0:
                load_w_bf16(wcb[j][:], bwcb[j], wv[:, :, 2048 + c * 128:2048 + (c + 1) * 128])
            for (c0, c1) in segs:
                bank = pring[0] % 2; pring[0] += 1
                rd = [bwcb[j]] + [bhT[t] for t in range(c0 // 128, c1 // 128)]
                for kc in range(8):
                    mm(pb[bank][:, 0:c1 - c0], wcb[j][:, kc, :], hT[:, kc, c0:c1], kc == 0, kc == 7, rd, [bpb[bank]], kc == 7)
                cp("act", raw[:, c0:c1], pb[bank][:, 0:c1 - c0], [bpb[bank]], [braw])

        def xbc_post1(c):
            j = c % 2
            raw = raws[j]; braw = braws[j]
            act(acc[:], raw[:], AF.Identity, [braw, bpc], [bacc], bias=pc[:, CB + c:CB + c + 1], scale=pc[:, CW + 3 * c + 1:CW + 3 * c + 2])

        def xbc_post(c):
            j = c % 2
            raw = raws[j]; braw = braws[j]
            stt("dve", acc[:, 1:n], raw[:, 0:n - 1], pc[:, CW + 3 * c:CW + 3 * c + 1], acc[:, 1:n], ALU.mult, ALU.add, [braw, bacc, bpc], [bacc])
            stt("dve", acc[:, 0:n - 1], raw[:, 1:n], pc[:, CW + 3 * c + 2:CW + 3 * c + 3], acc[:, 0:n - 1], ALU.mult, ALU.add, [braw, bacc, bpc], [bacc])
            if c < 4 or not own:
                dstv = cv[j][:]; bd = bcv[j]
            elif c < 6:
                dstv = BT[:, c - 4, :]; bd = bBT[c - 4]
            else:
                dstv = CT[:, c - 6, :]; bd = bCT[c - 6]
            act(dstv, acc[:], AF.Silu, [bacc], [bd])
            if c < 6:
                for t0 in range(0, ntile, 8):
                    tn = min(8, ntile - t0)
                    bank = 2 + (pring[0] % 2); pring[0] += 1
                    pvw = pbf(bank)
                    for t in range(tn):
                        trp(pvw[:, t * 128:(t + 1) * 128], dstv[:, (t0 + t) * 128:(t0 + t + 1) * 128], identb, [bd, bcmb], [bpb[bank]], t == tn - 1)
                    src = pvw[:, 0:tn * 128].rearrange("p (t c) -> p t c", c=128)
                    if c < 4:
                        cp("dve", x_tok[:, t0:t0 + tn, c * 128:(c + 1) * 128], src, [bpb[bank]], bxtok[t0:t0 + tn])
                    else:
                        cp("dve", B_tok[:, t0:t0 + tn, (c - 4) * 128:(c - 3) * 128], src, [bpb[bank]], bBtok[t0:t0 + tn])

        xbc_mm(0)
        for c in range(nch):
            xbc_post1(c)
            if c + 1 < nch:
                xbc_mm(c + 1)
            if bg:
                bg.pop(0)()
            xbc_post(c)
        for t in range(ntile):
            for kc in range(8):
                mm(pb[4][:, t * 8:(t + 1) * 8], hT[:, kc, t * 128:(t + 1) * 128], wdt[:, kc, :], kc == 0, kc == 7, [bhT[t], bwdt], [bpb[4]],
                   kc == 7 and t == ntile - 1)
        psd = pb[4][:, 0:ntile * 8].rearrange("p (t h) -> p t h", h=8)
        for d in range(2):
            tt("dve", dta[:, d, :, :], psd, pr[:, DTB + d * 8:DTB + (d + 1) * 8].unsqueeze(1).to_broadcast([128, ntile, 8]), ALU.add,
               [bpb[4], bpr], [bdta])
        dflat = dta[:].rearrange("p d t h -> p (d t h)")
        act(dflat, dflat, AF.Exp, [bdta], [bdta])
        act(dflat, dflat, AF.Ln, [bdta, bepsc], [bdta], bias=onec[:, 0:1])
        for d in range(2):
            tt("dve", aa[:, d, :, :], dta[:, d, :, :], sv[:, 40 + d * 8:48 + d * 8].unsqueeze(1).to_broadcast([128, ntile, 8]), ALU.mult,
               [bdta, bsv], [baa])
        for si, (mat, d) in enumerate(((TU, 0), (SL, 0), (ONES, 0), (TL, 1), (SU, 1), (ONES, 1))):
            bk = 5 + si // 3
            col = (si % 3) * nsc * 8
            mm(pb[bk][:, col:col + nsc * 8], mat, aa[:, d, 0:nsc, :], True, True, [bcm, baa], [bpb[bk]], si % 3 == 2)
        for half_ in range(2):
            act(EC[:, half_ * 3:half_ * 3 + 3, 0:nsc, :], pb[5 + half_][:, 0:3 * nsc * 8].rearrange("p (s t h) -> p s t h", s=3, h=8),
                AF.Exp, [bpb[5 + half_]], [bEC])
        k.release(mA)
        k.barrier()
        k.sb_end = topS
        if own:
            Hbb = k.sb("Hbb", [128, 17, 512], BF16); bHbb = [Buf(f"Hbb{i}") for i in range(17)]
            Hfa = k.sb("Hfa", [128, 17, 512], BF16); bHfa = [Buf(f"Hfa{i}") for i in range(17)]
        Yf = k.sb("Yf", [128, 8, 128], F32); Yb = k.sb("Yb", [128, 8, 128], F32); bYf = Buf("Yf"); bYb = Buf("Yb")
        Eb = k.sb("Eb", [128, 8, 128], BF16); bEb = Buf("Eb")
        cbm = k.sb("cbm", [128, 2, 2, 128], BF16); bcbm = Buf("cbm")
        Wms = [k.sb(f"Wm{i}", [128, 2, 8, 128], BF16) for i in range(2)]; bWms = [[Buf(f"Wf{i}"), Buf(f"Wb{i}")] for i in range(2)]
        xdts = [k.sb(f"xdt{i}", [128, 2, 512], BF16) for i in range(2)]; bxdts = [[Buf(f"xdtf{i}"), Buf(f"xdtb{i}")] for i in range(2)]
        xw = k.sb("xw", [128, 512], BF16); bxw = Buf("xw")
        dw = k.sb("dw", [128, 8], F32); bdw = Buf("dw")
        Hf = k.sb("Hf", [128, 512], F32); bHf = Buf("Hf")
        Hfb = k.sb("Hfb", [128, 512], BF16); bHfb = Buf("Hfb")
        y1 = k.sb("y1", [128, 512], F32); y2 = k.sb("y2", [128, 512], F32); by1 = Buf("y1"); by2 = Buf("y2")
        sz = k.sb("sz", [128, 512], F32); bsz = Buf("sz")
        gst = k.sb("gst", [128, 8], F32); bgst = Buf("gst")
        sjunk = k.sb("sjunk", [128, 256], BF16); bsj = Buf("sjunk")
        stok = k.sb("stok", [128, 512], BF16); bstok = Buf("stok")
        if own:
            Dm = k.sb("Dm", [128, 8, 128], BF16); bDm = Buf("Dm")
            for h in range(8):
                ts("dve", Dm[:, h, :], cm[:, 0, :], sv[:, 56 + h:57 + h], None, ALU.mult, None, [bcm, bsv], [bDm])

        def x3(ap):
            return ap.rearrange("p (h q) -> p h q", q=64)

        xw2 = k.sb("xw2", [128, 512], BF16); bxw2 = Buf("xw2")
        dw2 = k.sb("dw2", [128, 8], F32); bdw2 = Buf("dw2")

        def state_update(ci, d, H, bH):
            si = 1 if d == 0 else 4
            ti = 2 if d == 0 else 5
            xw_, bxw_, dw_, bdw_, bk = (xw, bxw, dw, bdw, 7) if d == 1 else (xw2, bxw2, dw2, bdw2, 6)
            tt("dve", dw_[:], dta[:, d, ci, :], EC[:, si, ci, :], ALU.mult, [bdta, bEC], [bdw_])
            tt("dve", x3(xw_[:]), x3(x_tok[:, ci, :]), bc3(dw_[:], 8, 64), ALU.mult, [bxtok[ci], bdw_], [bxw_])
            for g in range(2):
                mm(pb[bk][:, g * 256:(g + 1) * 256], B_tok[:, ci, g * 128:(g + 1) * 128], xw_[:, g * 256:(g + 1) * 256], True, True,
                   [bBtok[ci], bxw_], [bpb[bk]], g == 1)
            tt("dve", x3(H[:]), x3(H[:]), bc3(EC[:, ti, ci, :], 8, 64), ALU.mult, [bH, bEC], [bH])
            tt("dve", H[:], H[:], pb[bk][:, :], ALU.add, [bH, bpb[bk]], [bH])

        if not own:
            for ci in range(ntile - 1, 0, -1):
                state_update(ci, 1, Hb, bHb)
        else:
            k.op("dve", lambda e: e.memset(Hf[:], 0.0), [], [bHf])
            for step in range(17):
                cb_ = 16 - step
                cp("act", Hbb[:, cb_, :], Hb[:], [bHb], [bHbb[cb_]])
                if cb_ >= 1:
                    state_update(cb_, 1, Hb, bHb)
                cp("act", Hfa[:, step, :], Hf[:], [bHf], [bHfa[step]])
                if step < 16:
                    state_update(step, 0, Hf, bHf)
            def stage_a(ci):
                cs = slice(ci * 128, (ci + 1) * 128)
                Wm = Wms[ci % 2]; bWm = bWms[ci % 2]; xdt = xdts[ci % 2]; bxdt = bxdts[ci % 2]
                tt("dve", Yf[:], TU.unsqueeze(1).to_broadcast([128, 8, 128]), bc3(aa[:, 0, ci, :], 8, 128), ALU.mult, [bcm, baa], [bYf])
                tt("pool", Yb[:], TL.unsqueeze(1).to_broadcast([128, 8, 128]), bc3(aa[:, 1, ci, :], 8, 128), ALU.mult, [bcm, baa], [bYb])

            def stage_a1b(ci):
                cs = slice(ci * 128, (ci + 1) * 128)
                for hg in range(2):
                    mm(pb[hg][:, :], SL, Yf[:, hg * 4:(hg + 1) * 4, :], True, False, [bcm, bYf], [bpb[hg]], False)
                    mm(pb[hg][:, :], SU, Yb[:, hg * 4:(hg + 1) * 4, :], False, True, [bcm, bYb], [bpb[hg]], True)
                    act(Eb[:, hg * 4:(hg + 1) * 4, :], pb[hg][:, :].rearrange("p (h l) -> p h l", l=128), AF.Exp, [bpb[hg]], [bEb])
                for g in range(2):
                    mm(pb[2][:, g * 128:(g + 1) * 128], BT[:, g, cs], CT[:, g, cs], True, True, [bBT[g], bCT[g]], [bpb[2]], g == 1)

            def stage_a2(ci):
                cs = slice(ci * 128, (ci + 1) * 128)
                Wm = Wms[ci % 2]; bWm = bWms[ci % 2]; xdt = xdts[ci % 2]; bxdt = bxdts[ci % 2]
                for g in range(2):
                    tt("dve", cbm[:, g, 0, :], pb[2][:, g * 128:(g + 1) * 128], TU, ALU.mult, [bpb[2], bcm], [bcbm])
                    tt("dve", cbm[:, g, 1, :], pb[2][:, g * 128:(g + 1) * 128], TL, ALU.mult, [bpb[2], bcm], [bcbm])
                for g in range(2):
                    for d, eng in ((0, "dve"), (1, "pool")):
                        tt(eng, Wm[:, d, g * 4:(g + 1) * 4, :], Eb[:, g * 4:(g + 1) * 4, :],
                           cbm[:, g, d, :].unsqueeze(1).to_broadcast([128, 4, 128]), ALU.mult, [bEb, bcbm], [bWm[d]])
                for d in range(2):
                    tt("dve" if d == 0 else "pool", x3(xdt[:, d, :]), x3(x_tok[:, ci, :]), bc3(dta[:, d, ci, :], 8, 64), ALU.mult,
                       [bxtok[ci], bdta], [bxdt[d]])
            def stage_b(ci):
                cs = slice(ci * 128, (ci + 1) * 128)
                Wm = Wms[ci % 2]; bWm = bWms[ci % 2]; xdt = xdts[ci % 2]; bxdt = bxdts[ci % 2]
                for h in range(8):
                    for d in range(2):
                        mm(pb[3][:, h * 64:(h + 1) * 64], Wm[:, d, h, :], xdt[:, d, h * 64:(h + 1) * 64], d == 0, False,
                           [bWm[d], bxdt[d]], [bpb[3]], False)
                    mm(pb[3][:, h * 64:(h + 1) * 64], Dm[:, h, :], x_tok[:, ci, h * 64:(h + 1) * 64], False, True,
                       [bDm, bxtok[ci]], [bpb[3]], h == 7)

                for g in range(2):
                    mm(pb[4][:, g * 256:(g + 1) * 256], CT[:, g, cs], Hfa[:, ci, g * 256:(g + 1) * 256], True, True, [bCT[g], bHfa[ci]], [bpb[4]], g == 1)
                for g in range(2):
                    mm(pb[5][:, g * 256:(g + 1) * 256], CT[:, g, cs], Hbb[:, ci, g * 256:(g + 1) * 256], True, True, [bCT[g], bHbb[ci]], [bpb[5]], g == 1)

            def stage_b2(ci):
                cs = slice(ci * 128, (ci + 1) * 128)
                tt("dve", x3(y1[:]), x3(pb[4][:, :]), bc3(EC[:, 0, ci, :], 8, 64), ALU.mult, [bpb[4], bEC], [by1])
                tt("dve", x3(y2[:]), x3(pb[5][:, :]), bc3(EC[:, 3, ci, :], 8, 64), ALU.mult, [bpb[5], bEC], [by2])
                tt("dve", y1[:], y1[:], y2[:], ALU.add, [by1, by2], [by1])
                tt("dve", y1[:], y1[:], pb[3][:, :], ALU.add, [by1, bpb[3]], [by1])
                tt("dve", y1[:], y1[:], sza[:, ci, :], ALU.mult, [by1, bsza[ci]], [by1])
                k.op("dve", lambda e: e.memset(gst[:], 0.0), [], [bgst])
                for g in range(2):
                    act(sjunk[:], y1[:, g * 256:(g + 1) * 256], AF.Square, [by1], [bsj, bgst], accum_out=gst[:, g:g + 1])
                rstd_from_ssq(gst, bgst, 2, 1.0 / 256)
                for g in range(2):
                    stt("dve", stok[:, g * 256:(g + 1) * 256], y1[:, g * 256:(g + 1) * 256], gst[:, 4 + g:5 + g],
                        pr[:, SSMNW + g * 256:SSMNW + (g + 1) * 256], ALU.mult, ALU.mult, [by1, bgst, bpr], [bstok])
                pvw = pbf(7)
                for j in range(4):
                    trp(pvw[:, j * 128:(j + 1) * 128], stok[:, j * 128:(j + 1) * 128], identb, [bstok, bcmb], [bpb[7]], j == 3)
                cp("act", ssmT[:, :, cs], pvw[:, 0:512].rearrange("p (j c) -> p j c", c=128), [bpb[7]], [bssmT])
            NCH = 17
            stage_a(0); stage_a1b(0); stage_a2(0)
            stage_a(1); stage_a1b(1)
            for ci in range(NCH):
                if ci + 2 < NCH:
                    stage_a(ci + 2)
                stage_b(ci)
                if ci + 1 < NCH:
                    stage_a2(ci + 1)
                if ci + 2 < NCH:
                    stage_a1b(ci + 2)
                stage_b2(ci)
        k.release(mS)
        k.barrier()

    import os
    if not os.environ.get("SKIPSSD"):
        ssd_phase(16, 16, False)
        ssd_phase(0, 18, True)
    if debug:
        for j in range(4):
            k.dma("pool", st, dbg["ssm"][j * 128:(j + 1) * 128, :], ssmT[:, j, :], reads=[bssmT])
    if upto == "ssd":
        k.barrier()
        return nc, k

    mQ = k.mark()
    kT = k.sb("kT", [128, 4, T], BF16); bkT = [[Buf(f"kT{c}_{g}") for g in range(8)] for c in range(4)]
    qT = k.sb("qT", [128, 4, NQ], BF16); bqT = [[Buf(f"qT{c}_{g}") for g in range(5)] for c in range(4)]
    vA = k.sb("vA", [128, 32, 4, 130], BF16); bvA = [Buf(f"vA{t}") for t in range(32)]
    k.op("pool", lambda e: e.memset(vA[:].rearrange("p t h e -> p (t h e)"), 1.0), [], bvA)
    if upto == "q0":
        k.barrier(); return nc, k
    mR = k.mark()
    cosg = k.sb("cosg", [128, 512], F32); sing = k.sb("sing", [128, 512], F32); bcs = Buf("cossin")
    ang = k.sb("ang", [128, 512], F32); bang = Buf("ang")
    kk = k.sb("kk", [128, 512], F32); bkk = Buf("kk")
    wq = k.sb("wq", [128, 8, 512], BF16); wk = k.sb("wk", [128, 8, 512], BF16); wvv = k.sb("wvv", [128, 8, 512], BF16)
    bwq = Buf("wq"); bwk = Buf("wk"); bwvv = Buf("wvv")
    for wi, (wt, bw) in enumerate(((wk, bwk), (wvv, bwvv), (wq, bwq))):
        wi0 = (1, 2, 0)[wi]
        load_w_bf16(wt[:], bw, wv[:, :, wi0 * 512:(wi0 + 1) * 512])
    hTgs = [k.sb(f"hTg{i}", [128, 8, 512], BF16) for i in range(2)]; bhTgs = [Buf(f"hTg{i}") for i in range(2)]
    kraw = [k.sb(f"kraw{i}", [128, 512], BF16) for i in range(2)]; bkraw = [Buf(f"kraw{i}") for i in range(2)]
    rt1 = k.sb("rt1", [128, 512], F32); brt1 = Buf("rt1")
    rt2 = k.sb("rt2", [128, 512], F32); brt2 = Buf("rt2")
    rc = [0]
    if upto == "q1":
        k.barrier(); return nc, k
    for g in range(8):
        if upto == "q2" and g == 1:
            k.barrier(); return nc, k
        cols = slice(g * 512, (g + 1) * 512)
        k.dma("sp", ld, posi[:], pos_d[:, cols].to_broadcast([128, 512]), writes=[bposi])
        cp("dve", ang[:], posi[:], [bposi], [bang])
        ts("dve", ang[:], ang[:], pc[:, IFQ:IFQ + 1], None, ALU.mult, None, [bang, bpc], [bang])
        for dst, shift in ((sing, 0.0), (cosg, float(np.pi / 2))):
            if shift != 0.0:
                ts("dve", ang[:], ang[:], shift, None, ALU.add, None, [bang], [bang])
            ts("dve", kk[:], ang[:], float(1 / (2 * np.pi)), MAGIC, ALU.mult, ALU.add, [bang], [bkk])
            ts("dve", kk[:], kk[:], MAGIC, None, ALU.subtract, None, [bkk], [bkk])
            stt("dve", dst[:], kk[:], -C1, ang[:], ALU.mult, ALU.add, [bkk, bang], [bcs])
            stt("dve", dst[:], kk[:], -C2, dst[:], ALU.mult, ALU.add, [bkk, bcs], [bcs])
            ts("dve", dst[:], dst[:], float(-np.pi), float(np.pi), ALU.max, ALU.min, [bcs], [bcs])
            act(dst[:], dst[:], AF.Sin, [bcs], [bcs])
        if upto == "q2a":
            k.barrier(); return nc, k
        hTg = hTgs[g % 2]; bhTg = bhTgs[g % 2]
        k.dma("sp", ld, hTg[:], hTs_d[g].rearrange("p (c t) -> p c t", t=512), writes=[bhTg])
        if upto == "q2b":
            k.barrier(); return nc, k
        jobs = []
        for (wt, bw, dstT, bdl, ncol) in ((wk, bwk, kT, bkT, 512), (wq, bwq, qT, bqT, 512 if g < 4 else (128 if g == 4 else 0))):
            if ncol == 0: continue
            for c in range(4):
                jobs.append((wt, bw, dstT, bdl, ncol, c))

        def kq_mm(job):
            wt, bw, dstT, bdl, ncol, c = job
            j = rc[0] % 2; rc[0] += 1
            for kc in range(8):
                mm(pb[j][:, 0:ncol], wt[:, kc, c * 128:(c + 1) * 128], hTg[:, kc, 0:ncol], kc == 0, kc == 7, [bw, bhTg], [bpb[j]], kc == 7)
            cp("act", kraw[j][:, 0:ncol], pb[j][:, 0:ncol], [bpb[j]], [bkraw[j]])
            return j

        def kq_post(job, j):
            wt, bw, dstT, bdl, ncol, c = job
            dcols = slice(g * 512, g * 512 + ncol)
            mm(pb[2 + j][:, 0:ncol], Rmb, kraw[j][:, 0:ncol], True, True, [bcmb, bkraw[j]], [bpb[2 + j]], True)
            tt("dve", rt1[:, 0:ncol], pb[j][:, 0:ncol], cosg[:, 0:ncol], ALU.mult, [bpb[j], bcs], [brt1])
            tt("dve", rt2[:, 0:ncol], pb[2 + j][:, 0:ncol], sing[:, 0:ncol], ALU.mult, [bpb[2 + j], bcs], [brt2])
            tt("pool", dstT[:, c, dcols], rt1[:, 0:ncol], rt2[:, 0:ncol], ALU.add, [brt1, brt2], [bdl[c][g]])
        jprev = kq_mm(jobs[0])
        for ji, job in enumerate(jobs):
            jcur = jprev
            if ji + 1 < len(jobs):
                jprev = kq_mm(jobs[ji + 1])
            kq_post(job, jcur)
        if upto == "q2c":
            k.barrier(); return nc, k
        for i in range(4):
            j = 4 + (i % 2)
            t = g * 4 + i
            for kc in range(8):
                mm(pb[j][:, :], hTg[:, kc, i * 128:(i + 1) * 128], wvv[:, kc, :], kc == 0, kc == 7, [bhTg, bwvv], [bpb[j]], kc == 7)
            cp("act", vA[:, t, :, 0:128], pb[j][:, :].rearrange("p (h e) -> p h e", e=128), [bpb[j]], [bvA[t]])
    k.release(mR)
    k.barrier()
    if upto == "qproj":
        return nc, k
    h2T = k.sb_top("h2T", [128, 8, NQ], BF16); bh2T = [Buf(f"h2T{i}") for i in range(17)]
    attT = k.sb_top("attT", [128, 4, NQ], BF16); battT = Buf("attT")
    ada_alloc(7, 7, 1, 2, 1)
    AD["limit"] = NHB
    ada_dma()
    ada_slots = [0]
    PT = [k.sb(f"PT{i}", [128, 512], BF16) for i in range(4)]; bPT = [Buf(f"PT{i}") for i in range(4)]
    osb = [k.sb(f"osb{i}", [128, 128], F32) for i in range(4)]; bosb = [Buf(f"osb{i}") for i in range(4)]
    ast = k.sb("ast", [128, 16], F32); bast = [Buf(f"ast{i}") for i in range(4)]
    osum = k.sb("osum", [128, 2, 2, 392], F32); bosum = Buf("osum")
    zlhs = k.sb("zlhs", [128, 128], BF16); zrhs = k.sb("zrhs", [128, 392], BF16); bzz = Buf("zz")
    k.op("dve", lambda e: e.memset(zlhs[:], 0.0), [], [bzz])
    k.op("dve", lambda e: e.memset(zrhs[:], 0.0), [], [bzz])
    asq = k.sb("asq", [128, 128], F32); basq = Buf("asq")
    rs4 = k.sb("rs4", [128, 8], F32); brs4 = Buf("rs4")
    atok = k.sb("atok", [128, 512], BF16); batok = Buf("atok")
    sctr = [0]; pctr = [0]
    SB = (0, 1, 6)
    NPT = len(PT)
    pending = []

    def epi_part2(h, q0, nq, nqi):
        for qi in range(nqi):
            bb = qi // 2; oc = (qi % 2) * 256
            a4 = ast[:, qi * 4:(qi + 1) * 4]
            k.op("dve", lambda e: e.reciprocal(out=a4[:, 0:1], in_=osum[:, 0, bb, oc + 128:oc + 129]), [bosum], [bast[qi]])
            k.op("dve", lambda e: e.reciprocal(out=a4[:, 1:2], in_=osum[:, 1, bb, oc + 128:oc + 129]), [bosum], [bast[qi]])
            tt("dve", a4[:, 1:2], a4[:, 1:2], nlam, ALU.mult, [bast[qi], bsv], [bast[qi]])
            ts("dve", osb[qi][:], osum[:, 0, bb, oc:oc + 128], a4[:, 0:1], None, ALU.mult, None, [bosum, bast[qi]], [bosb[qi]])
            stt("dve", osb[qi][:], osum[:, 1, bb, oc:oc + 128], a4[:, 1:2], osb[qi][:], ALU.mult, ALU.add, [bosum, bast[qi], bosb[qi]], [bosb[qi]])
        for qi in range(nqi):
            tt("dve", asq[:], osb[qi][:], osb[qi][:], ALU.mult, [bosb[qi]], [basq])
            k.op("dve", lambda e: e.reduce_sum(out=rs4[:, qi:qi + 1], in_=asq[:], axis=AX.X), [basq], [brs4])

    def epi_part2b(h, q0, nq, nqi):
        act(rs4[:, 4:4 + nqi], rs4[:, 0:nqi], AF.Ln, [brs4, bepsc], [brs4], bias=epsc[:, 0:1], scale=1.0 / 128)
        act(rs4[:, 4:4 + nqi], rs4[:, 4:4 + nqi], AF.Exp, [brs4], [brs4], scale=-0.5)
        for qi in range(nqi):
            stt("dve", atok[:, qi * 128:(qi + 1) * 128], osb[qi][:], rs4[:, 4 + qi:5 + qi], sub8[:], ALU.mult, ALU.mult, [bosb[qi], brs4, bsub8], [batok])

    def epi_part3(h, q0, nq, nqi):
        pvw = pbf(7)
        for qi in range(nqi):
            trp(pvw[:, qi * 128:(qi + 1) * 128], atok[:, qi * 128:(qi + 1) * 128], identb, [batok, bcmb], [bpb[7]], qi == nqi - 1)
        cp("dve", attT[:, h, q0:q0 + nq], pvw[:, 0:nq], [bpb[7]], [battT])

    pending3 = []; pending2b = []
    for h in range(4):
        for qg in range(5):
            nq = 512 if qg < 4 else 128
            nqi = nq // 128
            q0 = qg * 512

            def s_mm(kt, sub):
                prt = slice(64 * sub, 64 * sub + 64)
                b = SB[sctr[0] % 3]; sctr[0] += 1
                mm(pb[b][:, 0:nq], kT[prt, h, kt * 128:(kt + 1) * 128], qT[prt, h, q0:q0 + nq], True, True,
                   [bkT[h][kt // 4], bqT[h][qg]], [bpb[b]], True)
                return b
            pend = [s_mm(0, 0), s_mm(0, 1)]
            for ob_ in range(2, 2 + 2 * 2):
                if (ob_ - 2) % 2 < (nqi + 1) // 2:
                    mm(pb[ob_][:, 0:385], zlhs[:], zrhs[:, 0:385], True, False, [bzz], [bpb[ob_]], False)
            for kt in range(32):
                cur = pend
                pjs = []
                for sub in range(2):
                    pj = pctr[0] % NPT; pctr[0] += 1
                    pjs.append(pj)
                    act(PT[pj][:, 0:nq], pb[cur[sub]][:, 0:nq], AF.Exp, [bpb[cur[sub]]], [bPT[pj]], scale=0.125)
                if kt < 31:
                    pend = [s_mm(kt + 1, 0), s_mm(kt + 1, 1)]
                for sub in range(2):
                    pj = pjs[sub]
                    for qi in range(nqi):
                        ob = 2 + sub * 2 + qi // 2
                        oc = (qi % 2) * 256
                        mm(pb[ob][:, oc:oc + 129], PT[pj][:, qi * 128:(qi + 1) * 128], vA[:, kt, h, 0:129], False,
                           kt == 31 and (qi % 2 == 1 or qi == nqi - 1),
                           [bPT[pj], bvA[kt]], [bpb[ob]], kt == 31 and qi == nqi - 1)
                if kt == 2 and pending:
                    pending.pop(0)()
                if kt in (4, 12, 20, 28) and ada_state["mm"] < NHB:
                    ada_half()
                if kt == 14 and pending2b:
                    pending2b.pop(0)()
                if kt == 24 and pending3:
                    pending3.pop(0)()
            for pl in (pending, pending2b, pending3):
                while pl:
                    pl.pop(0)()
            nbk = (nqi + 1) // 2
            for bb in range(nbk):
                for sub in range(2):
                    cp("dve", osum[:, sub, bb, 0:385], pb[2 + sub * 2 + bb][:, 0:385], [bpb[2 + sub * 2 + bb]], [bosum])
            pending.append(lambda h=h, q0=q0, nq=nq, nqi=nqi: epi_part2(h, q0, nq, nqi))
            pending2b.append(lambda h=h, q0=q0, nq=nq, nqi=nqi: epi_part2b(h, q0, nq, nqi))
            pending3.append(lambda h=h, q0=q0, nq=nq, nqi=nqi: epi_part3(h, q0, nq, nqi))
    for pl in (pending, pending2b, pending3):
        while pl:
            pl.pop(0)()
    ada_finish()
    k.release(mQ)
    k.barrier()
    ada_release()
    if debug:
        for j in range(4):
            k.dma("pool", st, dbg["att"][j * 128:(j + 1) * 128, :], attT[:, j, :], reads=[battT])
    if upto == "att":
        k.barrier()
        return nc, k

    mO = k.mark()
    ln_alloc()
    make_rows(s2c, sh2c)
    xtl = alloc_xtl()
    woutb = k.sb("woutb", [128, 8, D], BF16); bwoutb = Buf("woutb")
    wov = wout_d.rearrange("(kc p) n -> p kc n", p=128)
    for q2 in range(2):
        load_w_bf16(woutb[:, :, q2 * 512:(q2 + 1) * 512], bwoutb, wov[:, :, q2 * 512:(q2 + 1) * 512])
    x1t = [k.sb(f"x1t{i}", [128, D], F32) for i in range(8)]; bx1t = [Buf(f"x1t{i}") for i in range(8)]
    mtmp = k.sb("mtmp", [128, D], F32); bmtmp = Buf("mtmp")
    ltmp = k.sb("ltmp", [128, D], F32); bltmp = Buf("ltmp")
    ogroups = [list(range(g0, min(17, g0 + 4))) for g0 in range(0, 17, 4)]
    oring = [0]

    def o_stage_a(ids):
        xts = load_x_tiles(xtl, ids)
        tl = []
        for ii, t in enumerate(ids):
            cs = slice(t * 128, (t + 1) * 128)
            xi = oring[0] % 8; oring[0] += 1
            pbase = 2 * (xi % 2)
            for hf in range(2):
                bk = pbase + hf
                for kc in range(8):
                    lhs = attT[:, kc, cs] if kc < 4 else ssmT[:, kc - 4, cs]
                    mm(pb[bk][:, :], lhs, woutb[:, kc, hf * 512:(hf + 1) * 512], kc == 0, kc == 7, [battT, bssmT, bwoutb], [bpb[bk]], kc == 7)
                tt("dve", mtmp[:, hf * 512:(hf + 1) * 512], pb[bk][:, :], g1b[:, hf * 512:(hf + 1) * 512], ALU.mult, [bpb[bk], bgb], [bmtmp])
            tt("dve", x1t[xi][:], mtmp[:], xts[ii][0], ALU.add, [bmtmp, xts[ii][1]], [bx1t[xi]])
            if t < 16:
                k.dma("pool", st, x1s_d[cs, :], x1t[xi][:], reads=[bx1t[xi]])
                if debug:
                    k.dma("pool", st, dbg["x1"][cs, :], x1t[xi][:], reads=[bx1t[xi]])
            tl.append((x1t[xi][:], bx1t[xi]))
        return tl

    tl_next = o_stage_a(ogroups[0])
    for gi, ids in enumerate(ogroups):
        st_cur = ln_stage1(tl_next)
        if gi + 1 < len(ogroups):
            tl_next = o_stage_a(ogroups[gi + 1])
        ln_stage2(st_cur, lambda i, kc, ids=ids: (h2T[:, :, ids[i] * 128:(ids[i] + 1) * 128], bh2T[ids[i]]), tmp=(ltmp[:], bltmp))
    k.release(mO)
    k.barrier()

    k.release(mCore)
    k.sb_end += 4 * NQ * 2
    wfob = k.sb("wfob", [128, NFF, D], BF16); bwfob = [Buf(f"wfob{j}") for j in range(NFF)]
    actT = k.sb("actT", [128, NFF, 1024], BF16); bactT = [Buf(f"actT{j}") for j in range(NFF)]
    raw = k.sb("fraw", [128, 1026], F32); braw = Buf("fraw")
    acc = [k.sb(f"facc{i}", [128, 1024], F32) for i in range(2)]; bacc = [Buf(f"facc{i}") for i in range(2)]
    wfs = []; bwfs = []
    wgb = [k.sb(f"wgb{i}", [128, 8, 128], BF16) for i in range(2)]; bwgb = [Buf(f"wgb{i}") for i in range(2)]
    wub = [k.sb(f"wub{i}", [128, 8, 128], BF16) for i in range(2)]; bwub = [Buf(f"wub{i}") for i in range(2)]
    xr = [k.sb(f"xr{i}", [128, D], F32) for i in range(2)]; bxr = [Buf(f"xr{i}") for i in range(2)]
    ot = [k.sb(f"ot{i}", [128, D], F32) for i in range(2)]; bot = [Buf(f"ot{i}") for i in range(2)]
    fst = k.sb("fst", [128, 8], F32); bfst = Buf("fst")
    gsave = k.sb("gsave", [128, NFF], F32); bgsave = Buf("gsave")
    ln_alloc()
    fjunk = LN["xn"][0]; bfj = bxn[0]
    wfiv = wfi_d.rearrange("(kc p) n -> p kc n", p=128)
    wctr = [0]; octr = [0]
    gring = [0]
    for half in range(2):
        tstart = half * 1024
        lo = 1 if half == 0 else 0
        gsegs = [(1, 513), (513, 1025), (1025, 1026)]
        if half == 0:
            k.op("dve", lambda e: e.memset(raw[:, 0:1], 0.0), [], [braw])
        for j in range(NFF):
            wj = j % 2
            load_w_bf16(wgb[wj][:], bwgb[wj], wfiv[:, :, j * 128:(j + 1) * 128], None, [w[:] for w in wfs], bwfs, wctr)
            load_w_bf16(wub[wj][:], bwub[wj], wfiv[:, :, DFF + j * 128:DFF + (j + 1) * 128], None, [w[:] for w in wfs], bwfs, wctr)
            if half == 0:
                load_w_bf16(wfob[:, j, :], bwfob[j], wfo_d[j * 128:(j + 1) * 128, :])
            for (r0, r1) in gsegs:
                b = gring[0] % 2; gring[0] += 1
                c0 = tstart - 1 + r0; c1 = tstart - 1 + r1
                for kc in range(8):
                    mm(pb[b][:, 0:r1 - r0], wgb[wj][:, kc, :], h2T[:, kc, c0:c1], kc == 0, kc == 7, [bwgb[wj]] + bh2T, [bpb[b]], kc == 7)
                cp("act", raw[:, r0:r1], pb[b][:, 0:r1 - r0], [bpb[b]], [braw])
            if half == 0:
                cp("dve", gsave[:, j:j + 1], raw[:, 1024:1025], [braw], [bgsave])
            else:
                cp("dve", raw[:, 0:1], gsave[:, j:j + 1], [bgsave], [braw])
            aj = j % 2
            act(acc[aj][:], raw[:, 1:1025], AF.Identity, [braw, bpc], [bacc[aj]], bias=pc[:, FCB + j:FCB + j + 1],
                scale=pc[:, FCW + 3 * j + 1:FCW + 3 * j + 2])
            stt("dve", acc[aj][:], raw[:, 0:1024], pc[:, FCW + 3 * j:FCW + 3 * j + 1], acc[aj][:], ALU.mult, ALU.add, [braw, bacc[aj], bpc], [bacc[aj]])
            stt("dve", acc[aj][:], raw[:, 2:1026], pc[:, FCW + 3 * j + 2:FCW + 3 * j + 3], acc[aj][:], ALU.mult, ALU.add, [braw, bacc[aj], bpc], [bacc[aj]])
            act(acc[aj][:], acc[aj][:], AF.Silu, [bacc[aj]], [bacc[aj]])
            for ug in range(2):
                b = 2 + ug
                c0 = tstart + ug * 512
                for kc in range(8):
                    mm(pb[b][:, :], wub[wj][:, kc, :], h2T[:, kc, c0:c0 + 512], kc == 0, kc == 7, [bwub[wj]] + bh2T, [bpb[b]], kc == 7)
                tt("dve", actT[:, j, ug * 512:(ug + 1) * 512], pb[b][:, :], acc[aj][:, ug * 512:(ug + 1) * 512], ALU.mult,
                   [bpb[b], bacc[aj]], [bactT[j]])
        for ti in range(8):
            t = half * 8 + ti
            oj = ti % 2
            k.dma("sp", ld, xr[oj][:], x1s_d[t * 128:(t + 1) * 128, :], reads=[], writes=[bxr[oj]])
            for hf in range(2):
                b = 4 + hf + 2 * (ti % 2)
                for j in range(NFF):
                    mm(pb[b][:, :], actT[:, j, ti * 128:(ti + 1) * 128], wfob[:, j, hf * 512:(hf + 1) * 512], j == 0, j == NFF - 1,
                       [bactT[j], bwfob[j]], [bpb[b]], j == NFF - 1)
                tt("dve", ot[oj][:, hf * 512:(hf + 1) * 512], pb[b][:, :], g2b[:, hf * 512:(hf + 1) * 512], ALU.mult, [bpb[b], bgb], [bot[oj]])
            tt("dve", ot[oj][:], ot[oj][:], xr[oj][:], ALU.add, [bot[oj], bxr[oj]], [bot[oj]])
            k.op("dve", lambda e: e.memset(fst[:], 0.0), [], [bfst])
            act(fjunk[:], ot[oj][:], AF.Square, [bot[oj]], [bfj, bfst], accum_out=fst[:, 0:1])
            rstd_from_ssq(fst, bfst, 1, 1.0 / D)
            stt("dve", ot[oj][:], ot[oj][:], fst[:, 4:5], pr[:, FNW:FNW + D], ALU.mult, ALU.mult, [bot[oj], bfst, bpr], [bot[oj]])
            k.dma("sp", st, out_d[t * 128:(t + 1) * 128, :], ot[oj][:], reads=[bot[oj]])
    k.barrier()
    return nc, k


def _consts():
    i = np.arange(128)
    ident = np.eye(128, dtype=np.float32)
    TU = (i[:, None] <= i[None, :]).astype(np.float32)
    SL = (i[:, None] > i[None, :]).astype(np.float32)
    TL = (i[:, None] >= i[None, :]).astype(np.float32)
    SU = (i[:, None] < i[None, :]).astype(np.float32)
    ones = np.ones((128, 128), np.float32)
    Rm = np.zeros((128, 128), np.float32)
    for base in (0, 64):
        for d in range(8):
            Rm[base + d + 8, base + d] = -1.0
            Rm[base + d, base + d + 8] = 1.0
    cmat = np.stack([ident, TU, SL, TL, SU, ones, Rm], axis=1)
    inv_freq = (500000.0 ** (-np.arange(0, 16, 2, dtype=np.float32) / 16.0)).astype(np.float32)
    ifq = np.zeros(128, np.float32)
    for p in range(128):
        d = p % 64
        if d < 16:
            ifq[p] = inv_freq[d % 8]
    return np.ascontiguousarray(cmat), ifq


_CACHE = {}


def _col(v, n):
    return np.ascontiguousarray(np.asarray(v, np.float32).reshape(n, 128).T)


def make_in_maps(inputs):
    cmat, ifq = _consts()
    L = 0
    f = lambda name: np.asarray(inputs[name], np.float32)
    maps = []
    for core in range(8):
        b = core // 2; half = core % 2
        flip = half == 1
        x = f("x")[b]; pos = np.asarray(inputs["positions"], np.int32)[b]
        conv_w = f("conv_w")[L]; fconv_w = f("ffn_conv_w")[L]
        dtb = f("dt_bias")[L]; alog = f("a_log")[L]; dsk = f("d_skip")[L]
        if flip:
            x = x[::-1]; pos = pos[::-1]
            conv_w = conv_w[::-1]; fconv_w = fconv_w[::-1]
            dtb = dtb[::-1]; alog = alog[::-1]; dsk = dsk[::-1]
        pcol = np.zeros((128, NC), np.float32)
        pcol[:, N1W:N1W + 8] = _col(f("norm1_w")[L], 8)
        pcol[:, N2W:N2W + 8] = _col(f("norm2_w")[L], 8)
        for t in range(3):
            pcol[:, CW + t:CW + 24:3] = _col(conv_w[t], 8)
            pcol[:, FCW + t:FCW + 66:3] = _col(fconv_w[t], 22)
        pcol[:, CB:CB + 8] = _col(f("conv_b")[L], 8)
        pcol[:, FCB:FCB + 22] = _col(f("ffn_conv_b")[L], 22)
        pcol[:, CC:CC + 8] = _col(f("c")[b], 8)
        pcol[:, IFQ] = ifq
        prow = np.zeros((1, NR), np.float32)
        prow[0, SUBLN:SUBLN + 128] = f("subln_w")[L]
        prow[0, SSMNW:SSMNW + 512] = f("ssm_norm_w")[L]
        prow[0, FNW:FNW + D] = f("final_norm_w")
        prow[0, DTB:DTB + 16] = dtb.reshape(-1)
        prow[0, ALOG:ALOG + 16] = alog.reshape(-1)
        prow[0, DSK:DSK + 16] = dsk.reshape(-1)
        prow[0, LQ1:LQ1 + 64] = f("lambda_q1")[L]; prow[0, LK1:LK1 + 64] = f("lambda_k1")[L]
        prow[0, LQ2:LQ2 + 64] = f("lambda_q2")[L]; prow[0, LK2:LK2 + 64] = f("lambda_k2")[L]
        maps.append({
            "x": np.ascontiguousarray(x), "pos": np.ascontiguousarray(pos.reshape(1, T)),
            "w_ada": f("w_ada")[L], "b_ada": f("b_ada")[L].reshape(1, -1),
            "w_in": f("w_in")[L], "w_out": f("w_out")[L], "w_ffn_in": f("w_ffn_in")[L], "w_ffn_out": f("w_ffn_out")[L],
            "cmat": cmat, "pcol": pcol, "prow": prow,
        })
    return maps


def kernel(**inputs):
    if "nc" not in _CACHE:
        _CACHE["nc"] = build()[0]
    nc = _CACHE["nc"]
    maps = make_in_maps(inputs)
    res = run_bass_kernel_spmd(nc, maps, core_ids=list(range(8)))
    out = np.zeros((4, T, D), np.float32)
    for core in range(8):
        b = core // 2; half = core % 2
        o = np.asarray(res.results[core]["out"], np.float32)
        if half == 0:
            out[b, :NOWN] = o
        else:
            out[b, NOWN:] = o[::-1]
    return out
```

```python
import numpy as np
import concourse.bass as bass
import concourse.mybir as mybir
from concourse.bass_utils import run_bass_kernel_spmd

F32 = mybir.dt.float32; BF16 = mybir.dt.bfloat16; I32 = mybir.dt.int32
AF = mybir.ActivationFunctionType; ALU = mybir.AluOpType; AX = mybir.AxisListType
DTSIZE = {F32: 4, BF16: 2, I32: 4}
EPS = 1e-6
T = 4096; D = 1024; NOWN = 2048; NQ = 2176; DFF = 2816; NFF = 22
N1W, N2W, CW, CB, FCW, FCB, CC, IFQ, NC = 0, 8, 16, 40, 48, 114, 136, 144, 148
SUBLN, SSMNW, FNW, DTB, ALOG, DSK, LQ1, LK1, LQ2, LK2, NR = 0, 128, 640, 1664, 1680, 1696, 1712, 1776, 1840, 1904, 1968
MAGIC = float(1.5 * 2 ** 23); C1 = 6.28125; C2 = float(2 * np.pi - 6.28125)


class Buf:
    __slots__ = ("name", "lw", "rd", "ds", "excl")

    def __init__(self, name, excl=False):
        self.name = name; self.lw = None; self.rd = {}; self.ds = None
        self.excl = excl


class K:
    SB_BASE = 16512; SB_END = 229344

    def __init__(self):
        nc = self.nc = bass.Bass("TRN2", target_bir_lowering=False)
        self.eng = {"pe": nc.tensor, "act": nc.scalar, "dve": nc.vector, "pool": nc.gpsimd, "sp": nc.sync}
        self.sem = {}; self.cnt = {}
        for e in ("pe", "act", "dve", "pool"):
            self.sem[e] = nc.semaphore("s_" + e).__enter__(); self.cnt[e] = 0
        self.seen = {e: {} for e in self.eng}; self.pend = {}
        self.sb_ptr = self.SB_BASE; self.sb_end = self.SB_END; self.uid = 0; self.peak = 0

    def sb(self, name, shape, dt):
        per = int(np.prod(shape[1:])) * DTSIZE[dt]; per = (per + 31) // 32 * 32
        off = self.sb_ptr; self.sb_ptr += per
        assert self.sb_ptr <= self.sb_end, (name, self.sb_ptr - self.sb_end)
        self.peak = max(self.peak, self.sb_ptr)
        self.uid += 1
        return self.nc.alloc_sbuf_tensor_at(f"{name}_{self.uid}", list(shape), dt, offset=off)

    def sb_top(self, name, shape, dt):
        per = int(np.prod(shape[1:])) * DTSIZE[dt]; per = (per + 31) // 32 * 32
        self.sb_end -= per
        assert self.sb_ptr <= self.sb_end, (name, self.sb_ptr - self.sb_end)
        self.uid += 1
        return self.nc.alloc_sbuf_tensor_at(f"{name}_{self.uid}", list(shape), dt, offset=self.sb_end)

    def mark(self): return self.sb_ptr

    def release(self, m): self.sb_ptr = m

    def dsem(self, name):
        s = self.nc.semaphore("d_" + name).__enter__(); self.sem[name] = s; self.cnt[name] = 0; return name

    def _waits(self, e, reads, writes):
        need = {}

        def add(t):
            if t is None: return
            s, v = t
            if s == "pe" and e == "pe": return
            if need.get(s, 0) < v: need[s] = v
        for b in reads:
            add(b.lw)
            if b.excl:
                for s, t in b.rd.items():
                    if s != e: add(t)
        for b in writes:
            add(b.lw)
            for s, t in b.rd.items():
                add(t)
        sn = self.seen[e]
        for s, v in need.items():
            if sn.get(s, 0) < v:
                self.eng[e].wait_ge(self.sem[s], v); sn[s] = v

    def _done(self, ticket, reads, writes, eng):
        for b in writes: b.lw = ticket; b.rd = {}
        for b in reads: b.rd[eng] = ticket

    def op(self, e, ins_fn, reads=(), writes=(), inc=True):
        self._waits(e, reads, writes)
        ins = ins_fn(self.eng[e])
        pr_, pw_ = self.pend.setdefault(e, ([], []))
        if inc:
            self.cnt[e] += 1; ins.then_inc(self.sem[e], 1)
            self._done((e, self.cnt[e]), list(reads) + pr_, list(writes) + pw_, e)
            self.pend[e] = ([], [])
        else:
            for b in reads:
                if not any(b is x for x in pr_): pr_.append(b)
            for b in writes:
                if not any(b is x for x in pw_): pw_.append(b)
        return ins

    def dma(self, q, ds, out, in_, reads=(), writes=()):
        prim = writes[0] if len(writes) else reads[0]
        if prim.ds is None:
            prim.ds = self.dsem(f"b{len(self.sem)}")
        ds = prim.ds
        self._waits(q, reads, writes)
        ins = self.eng[q].dma_start(out=out, in_=in_)
        self.cnt[ds] += 16; ins.then_inc(self.sem[ds], 16)
        self._done((ds, self.cnt[ds]), reads, writes, "dma_" + ds)
        return ins

    def barrier(self, skip=()):
        tot = {s: c for s, c in self.cnt.items() if c > 0 and s not in skip}
        for e in self.eng:
            sn = self.seen[e]
            for s, v in tot.items():
                if s == e: continue
                if sn.get(s, 0) < v:
                    self.eng[e].wait_ge(self.sem[s], v); sn[s] = v


def build(debug=False, upto="all"):
    k = K(); nc = k.nc

    def din(name, shape, dt=F32): return nc.dram_tensor(name, list(shape), dt, kind="ExternalInput").ap()

    def dout(name, shape, dt=F32): return nc.dram_tensor(name, list(shape), dt, kind="ExternalOutput").ap()
    x_d = din("x", [T, D]); pos_d = din("pos", [1, T], I32)
    wada_d = din("w_ada", [D, 6 * D]); bada_d = din("b_ada", [1, 6 * D])
    win_d = din("w_in", [D, 3080]); wout_d = din("w_out", [D, D])
    wfi_d = din("w_ffn_in", [D, 2 * DFF]); wfo_d = din("w_ffn_out", [DFF, D])
    cmat_d = din("cmat", [128, 7, 128]); pcol_d = din("pcol", [128, NC]); prow_d = din("prow", [1, NR])
    out_d = dout("out", [NOWN, D])
    x1s_d = nc.dram_tensor("x1s", [NOWN, D], F32).ap()
    hTs_d = nc.dram_tensor("hTs", [8, 128, 8 * 512], BF16).ap()
    dbg = {}
    if debug:
        dbg["mod"] = dout("dbg_mod", [1, 6 * D])
        dbg["ssm"] = dout("dbg_ssm", [512, NQ], BF16)
        dbg["att"] = dout("dbg_att", [512, NQ], BF16)
        dbg["x1"] = dout("dbg_x1", [NOWN, D])
    wv = win_d.rearrange("(kc p) n -> p kc n", p=128)
    ld = k.dsem("ld"); lw = k.dsem("lw"); st = k.dsem("st")

    def act(out, in_, func, reads, writes, **kw):
        return k.op("act", lambda e: e.activation(out=out, in_=in_, func=func, **kw), reads, writes)

    def tt(eng, out, in0, in1, op, reads, writes):
        return k.op(eng, lambda e: e.tensor_tensor(out=out, in0=in0, in1=in1, op=op), reads, writes)

    def ts(eng, out, in0, s1, s2, op0, op1, reads, writes):
        if s2 is None:
            return k.op(eng, lambda e: e.tensor_scalar(out=out, in0=in0, scalar1=s1, scalar2=None, op0=op0), reads, writes)
        return k.op(eng, lambda e: e.tensor_scalar(out=out, in0=in0, scalar1=s1, scalar2=s2, op0=op0, op1=op1), reads, writes)

    def stt(eng, out, in0, scalar, in1, op0, op1, reads, writes):
        return k.op(eng, lambda e: e.scalar_tensor_tensor(out=out, in0=in0, scalar=scalar, in1=in1, op0=op0, op1=op1), reads, writes)

    def cp(eng, out, in_, reads, writes):
        if eng == "act":
            return act(out, in_, AF.Copy, reads, writes)
        return k.op(eng, lambda e: e.tensor_copy(out=out, in_=in_), reads, writes)

    def mm(out, lhsT, rhs, start, stop, reads, writes, inc):
        return k.op("pe", lambda e: e.matmul(out, lhsT=lhsT, rhs=rhs, start=start, stop=stop), reads, writes, inc=inc)

    def trp(out, in_, ident, reads, writes, inc):
        return k.op("pe", lambda e: e.transpose(out=out, in_=in_, identity=ident), reads, writes, inc=inc)

    def bc3(ap, a, b):
        return ap.unsqueeze(2).to_broadcast([128, a, b])

    pb = [nc.alloc_psum_tensor(f"pb{i}", [128, 512], F32) for i in range(8)]
    bpb = [Buf(f"pb{i}", excl=True) for i in range(8)]

    def pbf(i):
        return pb[i][:].bitcast(BF16)

    cm = k.sb("cm", [128, 7, 128], F32); bcm = Buf("cm")
    pc = k.sb("pc", [128, NC], F32); bpc = Buf("pc")
    pr = k.sb("pr", [128, NR], F32); bpr = Buf("pr")
    k.dma("sp", ld, cm[:], cmat_d, writes=[bcm])
    k.dma("sp", ld, pc[:], pcol_d, writes=[bpc])
    k.dma("sp", ld, pr[:], prow_d.to_broadcast([128, NR]), writes=[bpr])
    cmb = k.sb("cmb", [128, 4, 128], BF16); bcmb = Buf("cmb")
    for j, src in enumerate((0, 1, 3, 6)):
        cp("dve", cmb[:, j, :], cm[:, src, :], [bcm], [bcmb])
    identb = cmb[:, 0, :]; TUb = cmb[:, 1, :]; TLb = cmb[:, 2, :]; Rmb = cmb[:, 3, :]
    TU = cm[:, 1, :]; SL = cm[:, 2, :]; TL = cm[:, 3, :]; SU = cm[:, 4, :]; ONES = cm[:, 5, :]
    posi = k.sb("posi", [128, 512], I32); bposi = Buf("posi")
    sv = k.sb("sv", [128, 128], F32); bsv = Buf("sv")
    g1b = k.sb("g1b", [128, D], F32); g2b = k.sb("g2b", [128, D], F32); bgb = Buf("gb")
    srow = k.sb("srow", [128, D], F32); shrow = k.sb("shrow", [128, D], F32); bsrow = Buf("srow")
    dg = k.sb("dg", [128, 128], F32); bdg = Buf("dg")

    def make_rows(sc, shc):
        for which, (col, dst) in enumerate(((sc, srow), (shc, shrow))):
            for hf in range(2):
                for q in range(4):
                    kc = hf * 4 + q
                    ts("dve", dg[:], cm[:, 0, :], col[:, kc:kc + 1], None, ALU.mult, None, [bcm, bsv], [bdg])
                    mm(pb[3][:, q * 128:(q + 1) * 128], ONES, dg[:], True, True, [bcm, bdg], [bpb[3]], True)
                cp("dve", dst[:, hf * 512:(hf + 1) * 512], pb[3][:, :], [bpb[3]], [bsrow])

    modT = k.sb("modT", [128, 48], F32); bmodT = Buf("modT")
    wav = wada_d.rearrange("(kc p) n -> p kc n", p=128)
    NHB = 48
    NWAB = 4
    ada_state = {"dma": 0, "mm": 0}
    AD = {}

    def ada_alloc(bank, bank2, nwab=4, nbrow=4, nrow=2):
        AD["top"] = k.sb_end
        AD["cact"] = k.sb_top("cact", [128, 8], BF16); AD["bcact"] = Buf("cact")
        act(AD["cact"][:], pc[:, CC:CC + 8], AF.Silu, [bpc], [AD["bcact"]])
        AD["brow"] = [k.sb_top(f"brow{i}", [1, 512], F32) for i in range(nbrow)]; AD["bbrow"] = [Buf(f"brow{i}") for i in range(nbrow)]
        AD["row"] = [k.sb_top(f"arow{i}", [1, 512], F32) for i in range(nrow)]; AD["brw"] = [Buf(f"arow{i}") for i in range(nrow)]
        AD["nbrow"] = nbrow; AD["nrow"] = nrow
        AD["wab"] = [k.sb_top(f"wab{i}", [128, 8, 128], BF16) for i in range(nwab)]; AD["bwab"] = [Buf(f"wab{i}") for i in range(nwab)]
        AD["bank"] = bank; AD["bank2"] = bank2; AD["nwab"] = nwab
        print("ada_alloc free bytes:", k.sb_end - k.sb_ptr)

    def ada_release():
        k.sb_end = AD["top"]

    def ada_dma():
        i = ada_state["dma"]
        if i >= NHB: return
        ada_state["dma"] += 1
        j = i % AD["nwab"]
        if i % 4 == 0:
            cg = i // 4
            nb = AD["nbrow"]
            k.dma("sp", ld, AD["brow"][cg % nb][:], bada_d[:, cg * 512:(cg + 1) * 512], writes=[AD["bbrow"][cg % nb]])
        k.dma("pool", lw, AD["wab"][j][:], wav[:, :, i * 128:(i + 1) * 128], writes=[AD["bwab"][j]])

    def ada_half():
        i = ada_state["mm"]
        if i >= NHB: return
        ada_state["mm"] += 1
        while ada_state["dma"] <= i:
            ada_dma()
        j = i % AD["nwab"]; cg = i // 4; hh = i % 4
        bk = AD["bank"]
        for kc in range(8):
            mm(pb[bk][0:1, hh * 128:(hh + 1) * 128], AD["cact"][:, kc:kc + 1], AD["wab"][j][:, kc, :], kc == 0, kc == 7,
               [AD["bcact"], AD["bwab"][j]], [bpb[bk]], kc == 7)
        if ada_state["dma"] < min(NHB, AD["limit"]):
            ada_dma()
        row = AD["row"][cg % AD["nrow"]]; brw = AD["brw"][cg % AD["nrow"]]
        nb = AD["nbrow"]
        qs = slice(hh * 128, (hh + 1) * 128)
        tt("dve", row[0:1, qs], pb[bk][0:1, qs], AD["brow"][cg % nb][0:1, qs], ALU.add, [bpb[bk], AD["bbrow"][cg % nb]], [brw])
        if hh == 3:
            if debug:
                k.dma("pool", st, dbg["mod"][:, cg * 512:(cg + 1) * 512], row[0:1, :], reads=[brw])
            b2 = AD["bank2"]
            for q in range(4):
                mm(pb[b2][:, q:q + 1], row[0:1, q * 128:(q + 1) * 128], cm[0:1, 5, 0:1], True, True, [brw, bcm], [bpb[b2]], q == 3)
            cp("dve", modT[:, cg * 4:cg * 4 + 4], pb[b2][:, 0:4], [bpb[b2]], [bmodT])
            for gt, c0 in ((g1b, 2 * D), (g2b, 5 * D)):
                if c0 <= cg * 512 < c0 + D:
                    mm(pb[b2][:, :], cm[0:1, 5, :], row[0:1, :], True, True, [brw, bcm], [bpb[b2]], True)
                    cp("dve", gt[:, cg * 512 - c0:cg * 512 - c0 + 512], pb[b2][:, :], [bpb[b2]], [bgb])

    def ada_block():
        for _ in range(4):
            ada_half()

    ada_alloc(5, 2)
    AD["limit"] = 16
    for _ in range(NWAB):
        ada_dma()
    for cg in range(4):
        ada_block()
    ts("dve", sv[:, 64:72], modT[:, 8:16], 1.0, None, ALU.add, None, [bmodT], [bsv])
    tt("dve", sv[:, 0:8], sv[:, 64:72], pc[:, N1W:N1W + 8], ALU.mult, [bsv, bpc], [bsv])
    cp("dve", sv[:, 8:16], modT[:, 0:8], [bmodT], [bsv])

    def ada_finish():
        while ada_state["mm"] < NHB:
            ada_half()
        ts("dve", sv[:, 72:80], modT[:, 32:40], 1.0, None, ALU.add, None, [bmodT, bsv], [bsv])
        tt("dve", sv[:, 16:24], sv[:, 72:80], pc[:, N2W:N2W + 8], ALU.mult, [bsv, bpc], [bsv])
        cp("dve", sv[:, 24:32], modT[:, 24:32], [bmodT], [bsv])
    tt("dve", sv[:, 64:128], pr[:, LQ1:LQ1 + 64], pr[:, LK1:LK1 + 64], ALU.mult, [bpr, bsv], [bsv])
    k.op("dve", lambda e: e.reduce_sum(out=sv[:, 33:34], in_=sv[:, 64:128], axis=AX.X), [bsv], [bsv])
    tt("dve", sv[:, 64:128], pr[:, LQ2:LQ2 + 64], pr[:, LK2:LK2 + 64], ALU.mult, [bpr, bsv], [bsv])
    k.op("dve", lambda e: e.reduce_sum(out=sv[:, 34:35], in_=sv[:, 64:128], axis=AX.X), [bsv], [bsv])
    act(sv[:, 33:35], sv[:, 33:35], AF.Exp, [bsv], [bsv])
    tt("dve", sv[:, 32:33], sv[:, 34:35], sv[:, 33:34], ALU.subtract, [bsv], [bsv])
    ts("dve", sv[:, 32:33], sv[:, 32:33], -0.2, None, ALU.add, None, [bsv], [bsv])
    nlam = sv[:, 32:33]
    act(sv[:, 40:56], pr[:, ALOG:ALOG + 16], AF.Exp, [bpr, bsv], [bsv])
    ts("dve", sv[:, 40:56], sv[:, 40:56], -1.0, None, ALU.mult, None, [bsv], [bsv])
    tt("dve", sv[:, 56:64], pr[:, DSK:DSK + 8], pr[:, DSK + 8:DSK + 16], ALU.add, [bpr, bsv], [bsv])
    sub8 = k.sb("sub8", [128, 128], F32); bsub8 = Buf("sub8")
    ts("dve", sub8[:], pr[:, SUBLN:SUBLN + 128], 0.8, None, ALU.mult, None, [bpr], [bsub8])
    s1c = sv[:, 0:8]; sh1c = sv[:, 8:16]; s2c = sv[:, 16:24]; sh2c = sv[:, 24:32]
    make_rows(s1c, sh1c)

    LN = {}
    bxn = [Buf(f"xn{i}") for i in range(2)]

    def ln_alloc():
        LN["xn"] = [k.sb(f"xn{i}", [128, D], BF16) for i in range(2)]
        LN["sqj"] = k.sb("sqj", [128, D], BF16)
    stt_ = [k.sb(f"stat{i}", [128, 8], F32) for i in range(2)]; bstat = [Buf(f"stat{i}") for i in range(2)]
    lnc = [0, 0]
    bsqj = Buf("sqj")
    TRB = (6, 7)

    epsc = k.sb("epsc", [128, 1], F32); bepsc = Buf("epsc")
    k.op("dve", lambda e: e.memset(epsc[:], EPS), [], [bepsc])

    def rstd_from_ssq(stat, bst, n, inv_n):
        act(stat[:, 4:4 + n], stat[:, 0:n], AF.Ln, [bst, bepsc], [bst], bias=epsc[:, 0:1], scale=inv_n)
        act(stat[:, 4:4 + n], stat[:, 4:4 + n], AF.Exp, [bst], [bst], scale=-0.5)

    def ln_stage1(tiles):
        g = lnc[0] % 2; lnc[0] += 1
        stat = stt_[g]; bst = bstat[g]
        k.op("dve", lambda e: e.memset(stat[:], 0.0), [], [bst])
        tl = []
        for i, (xa, bx) in enumerate(tiles):
            act(LN["sqj"][:], xa, AF.Square, [bx], [bsqj, bst], accum_out=stat[:, i:i + 1])
            tl.append((xa, bx))
        rstd_from_ssq(stat, bst, len(tiles), 1.0 / D)
        return (tl, stat, bst)

    def ln_stage2(state, dst_fn, tmp=None):
        tl, stat, bst = state
        for i, (xa, bx) in enumerate(tl):
            j = lnc[1] % 2; lnc[1] += 1
            ta, bt = (xa, bx) if tmp is None else tmp
            stt("dve", ta, xa, stat[:, 4 + i:5 + i], srow[:], ALU.mult, ALU.mult, [bx, bst, bsrow], [bt])
            xn = LN["xn"]
            tt("dve", xn[j][:], ta, shrow[:], ALU.add, [bt, bsrow], [bxn[j]])
            bank = TRB[j]
            pv = pbf(bank)
            for kc in range(8):
                trp(pv[:, kc * 128:(kc + 1) * 128], xn[j][:, kc * 128:(kc + 1) * 128], identb, [bxn[j], bcmb], [bpb[bank]], kc == 7)
            d, bd = dst_fn(i, None)
            cp("act", d, pv[:, 0:1024].rearrange("p (c t) -> p c t", t=128), [bpb[bank]], [bd])

    def ln_group(tiles, dst_fn, sc, shc, tmp=None):
        ln_stage2(ln_stage1(tiles), dst_fn, tmp)

    bxtl = [Buf(f"xt{i}") for i in range(8)]
    xtc = [0]

    def alloc_xtl(nbuf=4):
        return [k.sb(f"xt{i}", [128, D], F32) for i in range(nbuf)]

    def load_x_tiles(xtl, tile_ids):
        res = []
        for t in tile_ids:
            i = xtc[0] % len(xtl); xtc[0] += 1
            k.dma("sp", ld, xtl[i][:], x_d[t * 128:(t + 1) * 128, :], writes=[bxtl[i]])
            res.append((xtl[i][:], bxtl[i]))
        return res

    def load_w_bf16(dst, bdst, src_ap, *unused, **unused_kw):
        k.dma("pool", lw, dst, src_ap, writes=[bdst])

    k.barrier()
    ada_release()
    mCore = k.mark()
    ssmT = k.sb("ssmT", [128, 4, NQ], BF16); bssmT = Buf("ssmT")
    Hb = k.sb("Hb", [128, 512], F32); bHb = Buf("Hb")
    k.op("dve", lambda e: e.memset(Hb[:], 0.0), [], [bHb])

    def ssd_phase(tile0, ntile, own, bg=None):
        n = ntile * 128
        nsc = 17 if own else ntile
        mS = k.mark()
        topS = k.sb_end
        ln_alloc()
        hT = k.sb_top("hT", [128, 8, n], BF16); bhT = [Buf(f"hT{i}") for i in range(ntile)]
        x_tok = k.sb("x_tok", [128, ntile, 512], BF16); bxtok = [Buf(f"xtok{i}") for i in range(ntile)]
        B_tok = k.sb("B_tok", [128, ntile, 256], BF16); bBtok = [Buf(f"Btok{i}") for i in range(ntile)]
        if own:
            BT = k.sb("BT", [128, 2, n], BF16); CT = k.sb("CT", [128, 2, n], BF16)
            bBT = [Buf("BT0"), Buf("BT1")]; bCT = [Buf("CT0"), Buf("CT1")]
            wz = k.sb_top("wz", [128, 8, 512], BF16); bwz = Buf("wz")
            sza = k.sb("sza", [128, 17, 512], BF16); bsza = [Buf(f"sza{i}") for i in range(17)]
        dta = k.sb("dta", [128, 2, ntile, 8], F32); bdta = Buf("dta")
        aa = k.sb("aa", [128, 2, ntile, 8], F32); baa = Buf("aa")
        EC = k.sb("EC", [128, 6, ntile, 8], F32); bEC = Buf("EC")
        wdt = k.sb("wdt", [128, 8, 8], BF16); bwdt = Buf("wdt")
        wcb = [k.sb(f"wcb{i}", [128, 8, 128], BF16) for i in range(2)]; bwcb = [Buf(f"wcb{i}") for i in range(2)]
        load_w_bf16(wcb[0][:], bwcb[0], wv[:, :, 2048:2048 + 128])
        load_w_bf16(wdt[:], bwdt, wv[:, :, 3072:3080])
        if own:
            load_w_bf16(wz[:], bwz, wv[:, :, 1536:2048])
        mA = k.mark()
        xtl = alloc_xtl(8)
        groups = [list(range(g0, min(ntile, g0 + 4))) for g0 in range(0, ntile, 4)]

        def hT_s1(ids):
            return ln_stage1(load_x_tiles(xtl, [tile0 + i for i in ids]))

        def hT_s2(ids, st_):
            ln_stage2(st_, lambda i, kc, ids=ids: (hT[:, :, ids[i] * 128:(ids[i] + 1) * 128], bhT[ids[i]]))
            g0 = ids[0]
            if own:
                while zq:
                    z_tile(zq.pop(0))
                zq.extend(ci for ci in ids if ci < 17)
            gq = (tile0 + g0) // 4
            if len(ids) == 4 and (own or gq >= 4) and not (own and gq >= 4):
                k.dma("pool", ld, hTs_d[gq].rearrange("p (c t) -> p c t", t=512), hT[:, :, g0 * 128:(g0 + 4) * 128],
                      reads=[bhT[i] for i in ids], writes=[bhts])
        zq = []

        def z_tile(ci):
            bk = 4 + (ci % 2)
            for kc in range(8):
                mm(pb[bk][:, :], hT[:, kc, ci * 128:(ci + 1) * 128], wz[:, kc, :], kc == 0, kc == 7, [bhT[ci], bwz], [bpb[bk]], kc == 7)
            act(sza[:, ci, :], pb[bk][:, :], AF.Silu, [bpb[bk]], [bsza[ci]])
        bhts = Buf("hTs_spill")
        st_next = hT_s1(groups[0])
        for gi, ids in enumerate(groups):
            st_cur = st_next
            if gi + 1 < len(groups):
                st_next = hT_s1(groups[gi + 1])
            hT_s2(ids, st_cur)
        while zq:
            z_tile(zq.pop(0))
        k.release(mA)
        k.barrier(skip=(bhts.ds,) if bhts.ds else ())
        raws = [k.sb(f"raw{i}", [128, n], F32) for i in range(2)]; braws = [Buf(f"raw{i}") for i in range(2)]
        acc = k.sb("acc", [128, n], F32); bacc = Buf("acc")
        cv = [k.sb("cv0", [128, n], BF16)] * 2; bcv = [Buf("cv0")] * 2
        wst = []; bwst = []
        wctr = [0]
        segs = [(c0, min(n, c0 + 512)) for c0 in range(0, n, 512)]
        pring = [0]
        nch = 8 if own else 6

        def xbc_mm(c):
            j = c % 2
            raw = raws[j]; braw = braws[j]
            if c > 0:
                load_w_bf16(wcb[j][:], bwcb[j], wv[:, :, 2048 + c * 128:2048 + (c + 1) * 128])
            for (c0, c1) in segs:
                bank = pring[0] % 2; pring[0] += 1
                rd = [bwcb[j]] + [bhT[t] for t in range(c0 // 128, c1 // 128)]
                for kc in range(8):
                    mm(pb[bank][:, 0:c1 - c0], wcb[j][:, kc, :], hT[:, kc, c0:c1], kc == 0, kc == 7, rd, [bpb[bank]], kc == 7)
                cp("act", raw[:, c0:c1], pb[bank][:, 0:c1 - c0], [bpb[bank]], [braw])

        def xbc_post1(c):
            j = c % 2
            raw = raws[j]; braw = braws[j]
            act(acc[:], raw[:], AF.Identity, [braw, bpc], [bacc], bias=pc[:, CB + c:CB + c + 1], scale=pc[:, CW + 3 * c + 1:CW + 3 * c + 2])

        def xbc_post(c):
            j = c % 2
            raw = raws[j]; braw = braws[j]
            stt("dve", acc[:, 1:n], raw[:, 0:n - 1], pc[:, CW + 3 * c:CW + 3 * c + 1], acc[:, 1:n], ALU.mult, ALU.add, [braw, bacc, bpc], [bacc])
            stt("dve", acc[:, 0:n - 1], raw[:, 1:n], pc[:, CW + 3 * c + 2:CW + 3 * c + 3], acc[:, 0:n - 1], ALU.mult, ALU.add, [braw, bacc, bpc], [bacc])
            if c < 4 or not own:
                dstv = cv[j][:]; bd = bcv[j]
            elif c < 6:
                dstv = BT[:, c - 4, :]; bd = bBT[c - 4]
            else:
                dstv = CT[:, c - 6, :]; bd = bCT[c - 6]
            act(dstv, acc[:], AF.Silu, [bacc], [bd])
            if c < 6:
                for t0 in range(0, ntile, 8):
                    tn = min(8, ntile - t0)
                    bank = 2 + (pring[0] % 2); pring[0] += 1
                    pvw = pbf(bank)
                    for t in range(tn):
                        trp(pvw[:, t * 128:(t + 1) * 128], dstv[:, (t0 + t) * 128:(t0 + t + 1) * 128], identb, [bd, bcmb], [bpb[bank]], t == tn - 1)
                    src = pvw[:, 0:tn * 128].rearrange("p (t c) -> p t c", c=128)
                    if c < 4:
                        cp("dve", x_tok[:, t0:t0 + tn, c * 128:(c + 1) * 128], src, [bpb[bank]], bxtok[t0:t0 + tn])
                    else:
                        cp("dve", B_tok[:, t0:t0 + tn, (c - 4) * 128:(c - 3) * 128], src, [bpb[bank]], bBtok[t0:t0 + tn])

        xbc_mm(0)
        for c in range(nch):
            xbc_post1(c)
            if c + 1 < nch:
                xbc_mm(c + 1)
            if bg:
                bg.pop(0)()
            xbc_post(c)
        for t in range(ntile):
            for kc in range(8):
                mm(pb[4][:, t * 8:(t + 1) * 8], hT[:, kc, t * 128:(t + 1) * 128], wdt[:, kc, :], kc == 0, kc == 7, [bhT[t], bwdt], [bpb[4]],
                   kc == 7 and t == ntile - 1)
        psd = pb[4][:, 0:ntile * 8].rearrange("p (t h) -> p t h", h=8)
        for d in range(2):
            tt("dve", dta[:, d, :, :], psd, pr[:, DTB + d * 8:DTB + (d + 1) * 8].unsqueeze(1).to_broadcast([128, ntile, 8]), ALU.add,
               [bpb[4], bpr], [bdta])
        dflat = dta[:].rearrange("p d t h -> p (d t h)")
        act(dflat, dflat, AF.Exp, [bdta], [bdta])
        ts("dve", dflat, dflat, 1.0, None, ALU.add, None, [bdta], [bdta])
        act(dflat, dflat, AF.Ln, [bdta], [bdta])
        for d in range(2):
            tt("dve", aa[:, d, :, :], dta[:, d, :, :], sv[:, 40 + d * 8:48 + d * 8].unsqueeze(1).to_broadcast([128, ntile, 8]), ALU.mult,
               [bdta, bsv], [baa])
        for si, (mat, d) in enumerate(((TU, 0), (SL, 0), (ONES, 0), (TL, 1), (SU, 1), (ONES, 1))):
            bk = 5 + si // 3
            col = (si % 3) * nsc * 8
            mm(pb[bk][:, col:col + nsc * 8], mat, aa[:, d, 0:nsc, :], True, True, [bcm, baa], [bpb[bk]], si % 3 == 2)
        for half_ in range(2):
            act(EC[:, half_ * 3:half_ * 3 + 3, 0:nsc, :], pb[5 + half_][:, 0:3 * nsc * 8].rearrange("p (s t h) -> p s t h", s=3, h=8),
                AF.Exp, [bpb[5 + half_]], [bEC])
        k.release(mA)
        k.barrier()
        k.sb_end = topS
        if own:
            Hbb = k.sb("Hbb", [128, 17, 512], BF16); bHbb = [Buf(f"Hbb{i}") for i in range(17)]
            Hfa = k.sb("Hfa", [128, 17, 512], BF16); bHfa = [Buf(f"Hfa{i}") for i in range(17)]
        Yf = k.sb("Yf", [128, 8, 128], F32); Yb = k.sb("Yb", [128, 8, 128], F32); bYf = Buf("Yf"); bYb = Buf("Yb")
        Eb = k.sb("Eb", [128, 8, 128], BF16); bEb = Buf("Eb")
        cbm = k.sb("cbm", [128, 2, 2, 128], BF16); bcbm = Buf("cbm")
        Wms = [k.sb(f"Wm{i}", [128, 2, 8, 128], BF16) for i in range(2)]; bWms = [[Buf(f"Wf{i}"), Buf(f"Wb{i}")] for i in range(2)]
        xdts = [k.sb(f"xdt{i}", [128, 2, 512], BF16) for i in range(2)]; bxdts = [[Buf(f"xdtf{i}"), Buf(f"xdtb{i}")] for i in range(2)]
        xw = k.sb("xw", [128, 512], BF16); bxw = Buf("xw")
        dw = k.sb("dw", [128, 8], F32); bdw = Buf("dw")
        Hf = k.sb("Hf", [128, 512], F32); bHf = Buf("Hf")
        Hfb = k.sb("Hfb", [128, 512], BF16); bHfb = Buf("Hfb")
        y1 = k.sb("y1", [128, 512], F32); y2 = k.sb("y2", [128, 512], F32); by1 = Buf("y1"); by2 = Buf("y2")
        sz = k.sb("sz", [128, 512], F32); bsz = Buf("sz")
        gst = k.sb("gst", [128, 8], F32); bgst = Buf("gst")
        sjunk = k.sb("sjunk", [128, 256], BF16); bsj = Buf("sjunk")
        stok = k.sb("stok", [128, 512], BF16); bstok = Buf("stok")
        if own:
            Dm = k.sb("Dm", [128, 8, 128], BF16); bDm = Buf("Dm")
            for h in range(8):
                ts("dve", Dm[:, h, :], cm[:, 0, :], sv[:, 56 + h:57 + h], None, ALU.mult, None, [bcm, bsv], [bDm])

        def x3(ap):
            return ap.rearrange("p (h q) -> p h q", q=64)

        xw2 = k.sb("xw2", [128, 512], BF16); bxw2 = Buf("xw2")
        dw2 = k.sb("dw2", [128, 8], F32); bdw2 = Buf("dw2")

        def state_update(ci, d, H, bH):
            si = 1 if d == 0 else 4
            ti = 2 if d == 0 else 5
            xw_, bxw_, dw_, bdw_, bk = (xw, bxw, dw, bdw, 7) if d == 1 else (xw2, bxw2, dw2, bdw2, 6)
            tt("dve", dw_[:], dta[:, d, ci, :], EC[:, si, ci, :], ALU.mult, [bdta, bEC], [bdw_])
            tt("dve", x3(xw_[:]), x3(x_tok[:, ci, :]), bc3(dw_[:], 8, 64), ALU.mult, [bxtok[ci], bdw_], [bxw_])
            for g in range(2):
                mm(pb[bk][:, g * 256:(g + 1) * 256], B_tok[:, ci, g * 128:(g + 1) * 128], xw_[:, g * 256:(g + 1) * 256], True, True,
                   [bBtok[ci], bxw_], [bpb[bk]], g == 1)
            tt("dve", x3(H[:]), x3(H[:]), bc3(EC[:, ti, ci, :], 8, 64), ALU.mult, [bH, bEC], [bH])
            tt("dve", H[:], H[:], pb[bk][:, :], ALU.add, [bH, bpb[bk]], [bH])

        if not own:
            for ci in range(ntile - 1, 0, -1):
                state_update(ci, 1, Hb, bHb)
        else:
            k.op("dve", lambda e: e.memset(Hf[:], 0.0), [], [bHf])
            for step in range(17):
                cb_ = 16 - step
                cp("act", Hbb[:, cb_, :], Hb[:], [bHb], [bHbb[cb_]])
                if cb_ >= 1:
                    state_update(cb_, 1, Hb, bHb)
                cp("act", Hfa[:, step, :], Hf[:], [bHf], [bHfa[step]])
                if step < 16:
                    state_update(step, 0, Hf, bHf)
            def stage_a(ci):
                cs = slice(ci * 128, (ci + 1) * 128)
                Wm = Wms[ci % 2]; bWm = bWms[ci % 2]; xdt = xdts[ci % 2]; bxdt = bxdts[ci % 2]
                tt("dve", Yf[:], TU.unsqueeze(1).to_broadcast([128, 8, 128]), bc3(aa[:, 0, ci, :], 8, 128), ALU.mult, [bcm, baa], [bYf])
                tt("pool", Yb[:], TL.unsqueeze(1).to_broadcast([128, 8, 128]), bc3(aa[:, 1, ci, :], 8, 128), ALU.mult, [bcm, baa], [bYb])

            def stage_a1b(ci):
                cs = slice(ci * 128, (ci + 1) * 128)
                for hg in range(2):
                    mm(pb[hg][:, :], SL, Yf[:, hg * 4:(hg + 1) * 4, :], True, False, [bcm, bYf], [bpb[hg]], False)
                    mm(pb[hg][:, :], SU, Yb[:, hg * 4:(hg + 1) * 4, :], False, True, [bcm, bYb], [bpb[hg]], True)
                    act(Eb[:, hg * 4:(hg + 1) * 4, :], pb[hg][:, :].rearrange("p (h l) -> p h l", l=128), AF.Exp, [bpb[hg]], [bEb])
                for g in range(2):
                    mm(pb[2][:, g * 128:(g + 1) * 128], BT[:, g, cs], CT[:, g, cs], True, True, [bBT[g], bCT[g]], [bpb[2]], g == 1)

            def stage_a2(ci):
                cs = slice(ci * 128, (ci + 1) * 128)
                Wm = Wms[ci % 2]; bWm = bWms[ci % 2]; xdt = xdts[ci % 2]; bxdt = bxdts[ci % 2]
                for g in range(2):
                    tt("dve", cbm[:, g, 0, :], pb[2][:, g * 128:(g + 1) * 128], TU, ALU.mult, [bpb[2], bcm], [bcbm])
                    tt("dve", cbm[:, g, 1, :], pb[2][:, g * 128:(g + 1) * 128], TL, ALU.mult, [bpb[2], bcm], [bcbm])
                for g in range(2):
                    for d, eng in ((0, "dve"), (1, "pool")):
                        tt(eng, Wm[:, d, g * 4:(g + 1) * 4, :], Eb[:, g * 4:(g + 1) * 4, :],
                           cbm[:, g, d, :].unsqueeze(1).to_broadcast([128, 4, 128]), ALU.mult, [bEb, bcbm], [bWm[d]])
                for d in range(2):
                    tt("dve" if d == 0 else "pool", x3(xdt[:, d, :]), x3(x_tok[:, ci, :]), bc3(dta[:, d, ci, :], 8, 64), ALU.mult,
                       [bxtok[ci], bdta], [bxdt[d]])
            def stage_b(ci):
                cs = slice(ci * 128, (ci + 1) * 128)
                Wm = Wms[ci % 2]; bWm = bWms[ci % 2]; xdt = xdts[ci % 2]; bxdt = bxdts[ci % 2]
                for h in range(8):
                    for d in range(2):
                        mm(pb[3][:, h * 64:(h + 1) * 64], Wm[:, d, h, :], xdt[:, d, h * 64:(h + 1) * 64], d == 0, False,
                           [bWm[d], bxdt[d]], [bpb[3]], False)
                    mm(pb[3][:, h * 64:(h + 1) * 64], Dm[:, h, :], x_tok[:, ci, h * 64:(h + 1) * 64], False, True,
                       [bDm, bxtok[ci]], [bpb[3]], h == 7)

                for g in range(2):
                    mm(pb[4][:, g * 256:(g + 1) * 256], CT[:, g, cs], Hfa[:, ci, g * 256:(g + 1) * 256], True, True, [bCT[g], bHfa[ci]], [bpb[4]], g == 1)
                for g in range(2):
                    mm(pb[5][:, g * 256:(g + 1) * 256], CT[:, g, cs], Hbb[:, ci, g * 256:(g + 1) * 256], True, True, [bCT[g], bHbb[ci]], [bpb[5]], g == 1)

            def stage_b2(ci):
                cs = slice(ci * 128, (ci + 1) * 128)
                tt("dve", x3(y1[:]), x3(pb[4][:, :]), bc3(EC[:, 0, ci, :], 8, 64), ALU.mult, [bpb[4], bEC], [by1])
                tt("dve", x3(y2[:]), x3(pb[5][:, :]), bc3(EC[:, 3, ci, :], 8, 64), ALU.mult, [bpb[5], bEC], [by2])
                tt("dve", y1[:], y1[:], y2[:], ALU.add, [by1, by2], [by1])
                tt("dve", y1[:], y1[:], pb[3][:, :], ALU.add, [by1, bpb[3]], [by1])
                tt("dve", y1[:], y1[:], sza[:, ci, :], ALU.mult, [by1, bsza[ci]], [by1])
                k.op("dve", lambda e: e.memset(gst[:], 0.0), [], [bgst])
                for g in range(2):
                    act(sjunk[:], y1[:, g * 256:(g + 1) * 256], AF.Square, [by1], [bsj, bgst], accum_out=gst[:, g:g + 1])
                rstd_from_ssq(gst, bgst, 2, 1.0 / 256)
                for g in range(2):
                    stt("dve", stok[:, g * 256:(g + 1) * 256], y1[:, g * 256:(g + 1) * 256], gst[:, 4 + g:5 + g],
                        pr[:, SSMNW + g * 256:SSMNW + (g + 1) * 256], ALU.mult, ALU.mult, [by1, bgst, bpr], [bstok])
                pvw = pbf(7)
                for j in range(4):
                    trp(pvw[:, j * 128:(j + 1) * 128], stok[:, j * 128:(j + 1) * 128], identb, [bstok, bcmb], [bpb[7]], j == 3)
                cp("act", ssmT[:, :, cs], pvw[:, 0:512].rearrange("p (j c) -> p j c", c=128), [bpb[7]], [bssmT])
            NCH = 17
            stage_a(0); stage_a1b(0); stage_a2(0)
            stage_a(1); stage_a1b(1)
            for ci in range(NCH):
                if ci + 2 < NCH:
                    stage_a(ci + 2)
                stage_b(ci)
                if ci + 1 < NCH:
                    stage_a2(ci + 1)
                if ci + 2 < NCH:
                    stage_a1b(ci + 2)
                stage_b2(ci)
        k.release(mS)
        k.barrier()

    import os
    if not os.environ.get("SKIPSSD"):
        ssd_phase(16, 16, False)
        ssd_phase(0, 18, True)
    if debug:
        for j in range(4):
            k.dma("pool", st, dbg["ssm"][j * 128:(j + 1) * 128, :], ssmT[:, j, :], reads=[bssmT])
    if upto == "ssd":
        k.barrier()
        return nc, k

    mQ = k.mark()
    kT = k.sb("kT", [128, 4, T], BF16); bkT = [[Buf(f"kT{c}_{g}") for g in range(8)] for c in range(4)]
    qT = k.sb("qT", [128, 4, NQ], BF16); bqT = [[Buf(f"qT{c}_{g}") for g in range(5)] for c in range(4)]
    vA = k.sb("vA", [128, 32, 4, 130], BF16); bvA = [Buf(f"vA{t}") for t in range(32)]
    k.op("pool", lambda e: e.memset(vA[:].rearrange("p t h e -> p (t h e)"), 1.0), [], bvA)
    if upto == "q0":
        k.barrier(); return nc, k
    mR = k.mark()
    cosg = k.sb("cosg", [128, 512], F32); sing = k.sb("sing", [128, 512], F32); bcs = Buf("cossin")
    ang = k.sb("ang", [128, 512], F32); bang = Buf("ang")
    kk = k.sb("kk", [128, 512], F32); bkk = Buf("kk")
    wq = k.sb("wq", [128, 8, 512], BF16); wk = k.sb("wk", [128, 8, 512], BF16); wvv = k.sb("wvv", [128, 8, 512], BF16)
    bwq = Buf("wq"); bwk = Buf("wk"); bwvv = Buf("wvv")
    for wi, (wt, bw) in enumerate(((wk, bwk), (wvv, bwvv), (wq, bwq))):
        wi0 = (1, 2, 0)[wi]
        load_w_bf16(wt[:], bw, wv[:, :, wi0 * 512:(wi0 + 1) * 512])
    hTgs = [k.sb(f"hTg{i}", [128, 8, 512], BF16) for i in range(2)]; bhTgs = [Buf(f"hTg{i}") for i in range(2)]
    kraw = [k.sb(f"kraw{i}", [128, 512], BF16) for i in range(2)]; bkraw = [Buf(f"kraw{i}") for i in range(2)]
    rt1 = k.sb("rt1", [128, 512], F32); brt1 = Buf("rt1")
    rt2 = k.sb("rt2", [128, 512], F32); brt2 = Buf("rt2")
    rc = [0]
    if upto == "q1":
        k.barrier(); return nc, k
    for g in range(8):
        if upto == "q2" and g == 1:
            k.barrier(); return nc, k
        cols = slice(g * 512, (g + 1) * 512)
        k.dma("sp", ld, posi[:], pos_d[:, cols].to_broadcast([128, 512]), writes=[bposi])
        cp("dve", ang[:], posi[:], [bposi], [bang])
        ts("dve", ang[:], ang[:], pc[:, IFQ:IFQ + 1], None, ALU.mult, None, [bang, bpc], [bang])
        for dst, shift in ((sing, 0.0), (cosg, float(np.pi / 2))):
            if shift != 0.0:
                ts("dve", ang[:], ang[:], shift, None, ALU.add, None, [bang], [bang])
            ts("dve", kk[:], ang[:], float(1 / (2 * np.pi)), MAGIC, ALU.mult, ALU.add, [bang], [bkk])
            ts("dve", kk[:], kk[:], MAGIC, None, ALU.subtract, None, [bkk], [bkk])
            stt("dve", dst[:], kk[:], -C1, ang[:], ALU.mult, ALU.add, [bkk, bang], [bcs])
            stt("dve", dst[:], kk[:], -C2, dst[:], ALU.mult, ALU.add, [bkk, bcs], [bcs])
            ts("dve", dst[:], dst[:], float(-np.pi), float(np.pi), ALU.max, ALU.min, [bcs], [bcs])
            act(dst[:], dst[:], AF.Sin, [bcs], [bcs])
        if upto == "q2a":
            k.barrier(); return nc, k
        hTg = hTgs[g % 2]; bhTg = bhTgs[g % 2]
        k.dma("sp", ld, hTg[:], hTs_d[g].rearrange("p (c t) -> p c t", t=512), writes=[bhTg])
        if upto == "q2b":
            k.barrier(); return nc, k
        jobs = []
        for (wt, bw, dstT, bdl, ncol) in ((wk, bwk, kT, bkT, 512), (wq, bwq, qT, bqT, 512 if g < 4 else (128 if g == 4 else 0))):
            if ncol == 0: continue
            for c in range(4):
                jobs.append((wt, bw, dstT, bdl, ncol, c))

        def kq_mm(job):
            wt, bw, dstT, bdl, ncol, c = job
            j = rc[0] % 2; rc[0] += 1
            for kc in range(8):
                mm(pb[j][:, 0:ncol], wt[:, kc, c * 128:(c + 1) * 128], hTg[:, kc, 0:ncol], kc == 0, kc == 7, [bw, bhTg], [bpb[j]], kc == 7)
            cp("act", kraw[j][:, 0:ncol], pb[j][:, 0:ncol], [bpb[j]], [bkraw[j]])
            return j

        def kq_post(job, j):
            wt, bw, dstT, bdl, ncol, c = job
            dcols = slice(g * 512, g * 512 + ncol)
            mm(pb[2 + j][:, 0:ncol], Rmb, kraw[j][:, 0:ncol], True, True, [bcmb, bkraw[j]], [bpb[2 + j]], True)
            tt("dve", rt1[:, 0:ncol], pb[j][:, 0:ncol], cosg[:, 0:ncol], ALU.mult, [bpb[j], bcs], [brt1])
            tt("dve", rt2[:, 0:ncol], pb[2 + j][:, 0:ncol], sing[:, 0:ncol], ALU.mult, [bpb[2 + j], bcs], [brt2])
            tt("dve", dstT[:, c, dcols], rt1[:, 0:ncol], rt2[:, 0:ncol], ALU.add, [brt1, brt2], [bdl[c][g]])
        jprev = kq_mm(jobs[0])
        for ji, job in enumerate(jobs):
            jcur = jprev
            if ji + 1 < len(jobs):
                jprev = kq_mm(jobs[ji + 1])
            kq_post(job, jcur)
        if upto == "q2c":
            k.barrier(); return nc, k
        for i in range(4):
            j = 4 + (i % 2)
            t = g * 4 + i
            for kc in range(8):
                mm(pb[j][:, :], hTg[:, kc, i * 128:(i + 1) * 128], wvv[:, kc, :], kc == 0, kc == 7, [bhTg, bwvv], [bpb[j]], kc == 7)
            cp("act", vA[:, t, :, 0:128], pb[j][:, :].rearrange("p (h e) -> p h e", e=128), [bpb[j]], [bvA[t]])
    k.release(mR)
    k.barrier()
    if upto == "qproj":
        return nc, k
    h2T = k.sb_top("h2T", [128, 8, NQ], BF16); bh2T = [Buf(f"h2T{i}") for i in range(17)]
    attT = k.sb_top("attT", [128, 4, NQ], BF16); battT = Buf("attT")
    ada_alloc(7, 7, 1, 2, 1)
    AD["limit"] = NHB
    ada_dma()
    ada_slots = [0]
    PT = [k.sb(f"PT{i}", [128, 512], BF16) for i in range(4)]; bPT = [Buf(f"PT{i}") for i in range(4)]
    osb = [k.sb(f"osb{i}", [128, 128], F32) for i in range(4)]; bosb = [Buf(f"osb{i}") for i in range(4)]
    ast = k.sb("ast", [128, 16], F32); bast = [Buf(f"ast{i}") for i in range(4)]
    osum = k.sb("osum", [128, 2, 2, 392], F32); bosum = Buf("osum")
    zlhs = k.sb("zlhs", [128, 128], BF16); zrhs = k.sb("zrhs", [128, 392], BF16); bzz = Buf("zz")
    k.op("dve", lambda e: e.memset(zlhs[:], 0.0), [], [bzz])
    k.op("dve", lambda e: e.memset(zrhs[:], 0.0), [], [bzz])
    asq = k.sb("asq", [128, 128], F32); basq = Buf("asq")
    rs4 = k.sb("rs4", [128, 8], F32); brs4 = Buf("rs4")
    atok = k.sb("atok", [128, 512], BF16); batok = Buf("atok")
    sctr = [0]; pctr = [0]
    SB = (0, 1, 6)
    NPT = len(PT)
    pending = []

    def epi_part2(h, q0, nq, nqi):
        for qi in range(nqi):
            bb = qi // 2; oc = (qi % 2) * 256
            a4 = ast[:, qi * 4:(qi + 1) * 4]
            k.op("dve", lambda e: e.reciprocal(out=a4[:, 0:1], in_=osum[:, 0, bb, oc + 128:oc + 129]), [bosum], [bast[qi]])
            k.op("dve", lambda e: e.reciprocal(out=a4[:, 1:2], in_=osum[:, 1, bb, oc + 128:oc + 129]), [bosum], [bast[qi]])
            tt("dve", a4[:, 1:2], a4[:, 1:2], nlam, ALU.mult, [bast[qi], bsv], [bast[qi]])
            ts("dve", osb[qi][:], osum[:, 0, bb, oc:oc + 128], a4[:, 0:1], None, ALU.mult, None, [bosum, bast[qi]], [bosb[qi]])
            stt("dve", osb[qi][:], osum[:, 1, bb, oc:oc + 128], a4[:, 1:2], osb[qi][:], ALU.mult, ALU.add, [bosum, bast[qi], bosb[qi]], [bosb[qi]])
        for qi in range(nqi):
            tt("dve", asq[:], osb[qi][:], osb[qi][:], ALU.mult, [bosb[qi]], [basq])
            k.op("dve", lambda e: e.reduce_sum(out=rs4[:, qi:qi + 1], in_=asq[:], axis=AX.X), [basq], [brs4])
        ts("dve", rs4[:, 4:4 + nqi], rs4[:, 0:nqi], 1.0 / 128, EPS, ALU.mult, ALU.add, [brs4], [brs4])

    def epi_part2b(h, q0, nq, nqi):
        act(rs4[:, 4:4 + nqi], rs4[:, 4:4 + nqi], AF.Ln, [brs4], [brs4])
        act(rs4[:, 4:4 + nqi], rs4[:, 4:4 + nqi], AF.Exp, [brs4], [brs4], scale=-0.5)
        for qi in range(nqi):
            stt("dve", atok[:, qi * 128:(qi + 1) * 128], osb[qi][:], rs4[:, 4 + qi:5 + qi], sub8[:], ALU.mult, ALU.mult, [bosb[qi], brs4, bsub8], [batok])

    def epi_part3(h, q0, nq, nqi):
        pvw = pbf(7)
        for qi in range(nqi):
            trp(pvw[:, qi * 128:(qi + 1) * 128], atok[:, qi * 128:(qi + 1) * 128], identb, [batok, bcmb], [bpb[7]], qi == nqi - 1)
        cp("dve", attT[:, h, q0:q0 + nq], pvw[:, 0:nq], [bpb[7]], [battT])

    pending3 = []; pending2b = []
    for h in range(4):
        for qg in range(5):
            nq = 512 if qg < 4 else 128
            nqi = nq // 128
            q0 = qg * 512

            def s_mm(kt, sub):
                prt = slice(64 * sub, 64 * sub + 64)
                b = SB[sctr[0] % 3]; sctr[0] += 1
                mm(pb[b][:, 0:nq], kT[prt, h, kt * 128:(kt + 1) * 128], qT[prt, h, q0:q0 + nq], True, True,
                   [bkT[h][kt // 4], bqT[h][qg]], [bpb[b]], True)
                return b
            pend = [s_mm(0, 0), s_mm(0, 1)]
            for ob_ in range(2, 2 + 2 * 2):
                if (ob_ - 2) % 2 < (nqi + 1) // 2:
                    mm(pb[ob_][:, 0:385], zlhs[:], zrhs[:, 0:385], True, False, [bzz], [bpb[ob_]], False)
            for kt in range(32):
                cur = pend
                pjs = []
                for sub in range(2):
                    pj = pctr[0] % NPT; pctr[0] += 1
                    pjs.append(pj)
                    act(PT[pj][:, 0:nq], pb[cur[sub]][:, 0:nq], AF.Exp, [bpb[cur[sub]]], [bPT[pj]], scale=0.125)
                if kt < 31:
                    pend = [s_mm(kt + 1, 0), s_mm(kt + 1, 1)]
                for sub in range(2):
                    pj = pjs[sub]
                    for qi in range(nqi):
                        ob = 2 + sub * 2 + qi // 2
                        oc = (qi % 2) * 256
                        mm(pb[ob][:, oc:oc + 129], PT[pj][:, qi * 128:(qi + 1) * 128], vA[:, kt, h, 0:129], False,
                           kt == 31 and (qi % 2 == 1 or qi == nqi - 1),
                           [bPT[pj], bvA[kt]], [bpb[ob]], kt == 31 and qi == nqi - 1)
                if kt == 2 and pending:
                    pending.pop(0)()
                if kt in (4, 12, 20, 28) and ada_state["mm"] < NHB:
                    ada_half()
                if kt == 14 and pending2b:
                    pending2b.pop(0)()
                if kt == 24 and pending3:
                    pending3.pop(0)()
            for pl in (pending, pending2b, pending3):
                while pl:
                    pl.pop(0)()
            nbk = (nqi + 1) // 2
            for bb in range(nbk):
                for sub in range(2):
                    cp("dve", osum[:, sub, bb, 0:385], pb[2 + sub * 2 + bb][:, 0:385], [bpb[2 + sub * 2 + bb]], [bosum])
            pending.append(lambda h=h, q0=q0, nq=nq, nqi=nqi: epi_part2(h, q0, nq, nqi))
            pending2b.append(lambda h=h, q0=q0, nq=nq, nqi=nqi: epi_part2b(h, q0, nq, nqi))
            pending3.append(lambda h=h, q0=q0, nq=nq, nqi=nqi: epi_part3(h, q0, nq, nqi))
    for pl in (pending, pending2b, pending3):
        while pl:
            pl.pop(0)()
    ada_finish()
    k.release(mQ)
    k.barrier()
    ada_release()
    if debug:
        for j in range(4):
            k.dma("pool", st, dbg["att"][j * 128:(j + 1) * 128, :], attT[:, j, :], reads=[battT])
    if upto == "att":
        k.barrier()
        return nc, k

    mO = k.mark()
    ln_alloc()
    make_rows(s2c, sh2c)
    xtl = alloc_xtl()
    woutb = k.sb("woutb", [128, 8, D], BF16); bwoutb = Buf("woutb")
    wov = wout_d.rearrange("(kc p) n -> p kc n", p=128)
    for q2 in range(2):
        load_w_bf16(woutb[:, :, q2 * 512:(q2 + 1) * 512], bwoutb, wov[:, :, q2 * 512:(q2 + 1) * 512])
    x1t = [k.sb(f"x1t{i}", [128, D], F32) for i in range(8)]; bx1t = [Buf(f"x1t{i}") for i in range(8)]
    mtmp = k.sb("mtmp", [128, D], F32); bmtmp = Buf("mtmp")
    ltmp = k.sb("ltmp", [128, D], F32); bltmp = Buf("ltmp")
    ogroups = [list(range(g0, min(17, g0 + 4))) for g0 in range(0, 17, 4)]
    oring = [0]

    def o_stage_a(ids):
        xts = load_x_tiles(xtl, ids)
        tl = []
        for ii, t in enumerate(ids):
            cs = slice(t * 128, (t + 1) * 128)
            xi = oring[0] % 8; oring[0] += 1
            pbase = 2 * (xi % 2)
            for hf in range(2):
                bk = pbase + hf
                for kc in range(8):
                    lhs = attT[:, kc, cs] if kc < 4 else ssmT[:, kc - 4, cs]
                    mm(pb[bk][:, :], lhs, woutb[:, kc, hf * 512:(hf + 1) * 512], kc == 0, kc == 7, [battT, bssmT, bwoutb], [bpb[bk]], kc == 7)
                tt("dve", mtmp[:, hf * 512:(hf + 1) * 512], pb[bk][:, :], g1b[:, hf * 512:(hf + 1) * 512], ALU.mult, [bpb[bk], bgb], [bmtmp])
            tt("dve", x1t[xi][:], mtmp[:], xts[ii][0], ALU.add, [bmtmp, xts[ii][1]], [bx1t[xi]])
            if t < 16:
                k.dma("pool", st, x1s_d[cs, :], x1t[xi][:], reads=[bx1t[xi]])
                if debug:
                    k.dma("pool", st, dbg["x1"][cs, :], x1t[xi][:], reads=[bx1t[xi]])
            tl.append((x1t[xi][:], bx1t[xi]))
        return tl

    tl_next = o_stage_a(ogroups[0])
    for gi, ids in enumerate(ogroups):
        st_cur = ln_stage1(tl_next)
        if gi + 1 < len(ogroups):
            tl_next = o_stage_a(ogroups[gi + 1])
        ln_stage2(st_cur, lambda i, kc, ids=ids: (h2T[:, :, ids[i] * 128:(ids[i] + 1) * 128], bh2T[ids[i]]), tmp=(ltmp[:], bltmp))
    k.release(mO)
    k.barrier()

    k.release(mCore)
    k.sb_end += 4 * NQ * 2
    wfob = k.sb("wfob", [128, NFF, D], BF16); bwfob = [Buf(f"wfob{j}") for j in range(NFF)]
    actT = k.sb("actT", [128, NFF, 1024], BF16); bactT = [Buf(f"actT{j}") for j in range(NFF)]
    raw = k.sb("fraw", [128, 1026], F32); braw = Buf("fraw")
    acc = [k.sb(f"facc{i}", [128, 1024], F32) for i in range(2)]; bacc = [Buf(f"facc{i}") for i in range(2)]
    wfs = []; bwfs = []
    wgb = [k.sb(f"wgb{i}", [128, 8, 128], BF16) for i in range(2)]; bwgb = [Buf(f"wgb{i}") for i in range(2)]
    wub = [k.sb(f"wub{i}", [128, 8, 128], BF16) for i in range(2)]; bwub = [Buf(f"wub{i}") for i in range(2)]
    xr = [k.sb(f"xr{i}", [128, D], F32) for i in range(2)]; bxr = [Buf(f"xr{i}") for i in range(2)]
    ot = [k.sb(f"ot{i}", [128, D], F32) for i in range(2)]; bot = [Buf(f"ot{i}") for i in range(2)]
    fst = k.sb("fst", [128, 8], F32); bfst = Buf("fst")
    gsave = k.sb("gsave", [128, NFF], F32); bgsave = Buf("gsave")
    ln_alloc()
    fjunk = LN["xn"][0]; bfj = bxn[0]
    wfiv = wfi_d.rearrange("(kc p) n -> p kc n", p=128)
    wctr = [0]; octr = [0]
    gring = [0]
    for half in range(2):
        tstart = half * 1024
        lo = 1 if half == 0 else 0
        gsegs = [(1, 513), (513, 1025), (1025, 1026)]
        if half == 0:
            k.op("dve", lambda e: e.memset(raw[:, 0:1], 0.0), [], [braw])
        for j in range(NFF):
            wj = j % 2
            load_w_bf16(wgb[wj][:], bwgb[wj], wfiv[:, :, j * 128:(j + 1) * 128], None, [w[:] for w in wfs], bwfs, wctr)
            load_w_bf16(wub[wj][:], bwub[wj], wfiv[:, :, DFF + j * 128:DFF + (j + 1) * 128], None, [w[:] for w in wfs], bwfs, wctr)
            if half == 0:
                load_w_bf16(wfob[:, j, :], bwfob[j], wfo_d[j * 128:(j + 1) * 128, :])
            for (r0, r1) in gsegs:
                b = gring[0] % 2; gring[0] += 1
                c0 = tstart - 1 + r0; c1 = tstart - 1 + r1
                for kc in range(8):
                    mm(pb[b][:, 0:r1 - r0], wgb[wj][:, kc, :], h2T[:, kc, c0:c1], kc == 0, kc == 7, [bwgb[wj]] + bh2T, [bpb[b]], kc == 7)
                cp("act", raw[:, r0:r1], pb[b][:, 0:r1 - r0], [bpb[b]], [braw])
            if half == 0:
                cp("dve", gsave[:, j:j + 1], raw[:, 1024:1025], [braw], [bgsave])
            else:
                cp("dve", raw[:, 0:1], gsave[:, j:j + 1], [bgsave], [braw])
            aj = j % 2
            act(acc[aj][:], raw[:, 1:1025], AF.Identity, [braw, bpc], [bacc[aj]], bias=pc[:, FCB + j:FCB + j + 1],
                scale=pc[:, FCW + 3 * j + 1:FCW + 3 * j + 2])
            stt("dve", acc[aj][:], raw[:, 0:1024], pc[:, FCW + 3 * j:FCW + 3 * j + 1], acc[aj][:], ALU.mult, ALU.add, [braw, bacc[aj], bpc], [bacc[aj]])
            stt("dve", acc[aj][:], raw[:, 2:1026], pc[:, FCW + 3 * j + 2:FCW + 3 * j + 3], acc[aj][:], ALU.mult, ALU.add, [braw, bacc[aj], bpc], [bacc[aj]])
            act(acc[aj][:], acc[aj][:], AF.Silu, [bacc[aj]], [bacc[aj]])
            for ug in range(2):
                b = 2 + ug
                c0 = tstart + ug * 512
                for kc in range(8):
                    mm(pb[b][:, :], wub[wj][:, kc, :], h2T[:, kc, c0:c0 + 512], kc == 0, kc == 7, [bwub[wj]] + bh2T, [bpb[b]], kc == 7)
                tt("dve", actT[:, j, ug * 512:(ug + 1) * 512], pb[b][:, :], acc[aj][:, ug * 512:(ug + 1) * 512], ALU.mult,
                   [bpb[b], bacc[aj]], [bactT[j]])
        for ti in range(8):
            t = half * 8 + ti
            oj = ti % 2
            k.dma("sp", ld, xr[oj][:], x1s_d[t * 128:(t + 1) * 128, :], reads=[], writes=[bxr[oj]])
            for hf in range(2):
                b = 4 + hf + 2 * (ti % 2)
                for j in range(NFF):
                    mm(pb[b][:, :], actT[:, j, ti * 128:(ti + 1) * 128], wfob[:, j, hf * 512:(hf + 1) * 512], j == 0, j == NFF - 1,
                       [bactT[j], bwfob[j]], [bpb[b]], j == NFF - 1)
                tt("dve", ot[oj][:, hf * 512:(hf + 1) * 512], pb[b][:, :], g2b[:, hf * 512:(hf + 1) * 512], ALU.mult, [bpb[b], bgb], [bot[oj]])
            tt("dve", ot[oj][:], ot[oj][:], xr[oj][:], ALU.add, [bot[oj], bxr[oj]], [bot[oj]])
            k.op("dve", lambda e: e.memset(fst[:], 0.0), [], [bfst])
            act(fjunk[:], ot[oj][:], AF.Square, [bot[oj]], [bfj, bfst], accum_out=fst[:, 0:1])
            rstd_from_ssq(fst, bfst, 1, 1.0 / D)
            stt("dve", ot[oj][:], ot[oj][:], fst[:, 4:5], pr[:, FNW:FNW + D], ALU.mult, ALU.mult, [bot[oj], bfst, bpr], [bot[oj]])
            k.dma("sp", st, out_d[t * 128:(t + 1) * 128, :], ot[oj][:], reads=[bot[oj]])
    k.barrier()
    return nc, k


def _consts():
    i = np.arange(128)
    ident = np.eye(128, dtype=np.float32)
    TU = (i[:, None] <= i[None, :]).astype(np.float32)
    SL = (i[:, None] > i[None, :]).astype(np.float32)
    TL = (i[:, None] >= i[None, :]).astype(np.float32)
    SU = (i[:, None] < i[None, :]).astype(np.float32)
    ones = np.ones((128, 128), np.float32)
    Rm = np.zeros((128, 128), np.float32)
    for base in (0, 64):
        for d in range(8):
            Rm[base + d + 8, base + d] = -1.0
            Rm[base + d, base + d + 8] = 1.0
    cmat = np.stack([ident, TU, SL, TL, SU, ones, Rm], axis=1)
    inv_freq = (500000.0 ** (-np.arange(0, 16, 2, dtype=np.float32) / 16.0)).astype(np.float32)
    ifq = np.zeros(128, np.float32)
    for p in range(128):
        d = p % 64
        if d < 16:
            ifq[p] = inv_freq[d % 8]
    return np.ascontiguousarray(cmat), ifq


_CACHE = {}


def _col(v, n):
    return np.ascontiguousarray(np.asarray(v, np.float32).reshape(n, 128).T)


def make_in_maps(inputs):
    cmat, ifq = _consts()
    L = 0
    f = lambda name: np.asarray(inputs[name], np.float32)
    maps = []
    for core in range(8):
        b = core // 2; half = core % 2
        flip = half == 1
        x = f("x")[b]; pos = np.asarray(inputs["positions"], np.int32)[b]
        conv_w = f("conv_w")[L]; fconv_w = f("ffn_conv_w")[L]
        dtb = f("dt_bias")[L]; alog = f("a_log")[L]; dsk = f("d_skip")[L]
        if flip:
            x = x[::-1]; pos = pos[::-1]
            conv_w = conv_w[::-1]; fconv_w = fconv_w[::-1]
            dtb = dtb[::-1]; alog = alog[::-1]; dsk = dsk[::-1]
        pcol = np.zeros((128, NC), np.float32)
        pcol[:, N1W:N1W + 8] = _col(f("norm1_w")[L], 8)
        pcol[:, N2W:N2W + 8] = _col(f("norm2_w")[L], 8)
        for t in range(3):
            pcol[:, CW + t:CW + 24:3] = _col(conv_w[t], 8)
            pcol[:, FCW + t:FCW + 66:3] = _col(fconv_w[t], 22)
        pcol[:, CB:CB + 8] = _col(f("conv_b")[L], 8)
        pcol[:, FCB:FCB + 22] = _col(f("ffn_conv_b")[L], 22)
        pcol[:, CC:CC + 8] = _col(f("c")[b], 8)
        pcol[:, IFQ] = ifq
        prow = np.zeros((1, NR), np.float32)
        prow[0, SUBLN:SUBLN + 128] = f("subln_w")[L]
        prow[0, SSMNW:SSMNW + 512] = f("ssm_norm_w")[L]
        prow[0, FNW:FNW + D] = f("final_norm_w")
        prow[0, DTB:DTB + 16] = dtb.reshape(-1)
        prow[0, ALOG:ALOG + 16] = alog.reshape(-1)
        prow[0, DSK:DSK + 16] = dsk.reshape(-1)
        prow[0, LQ1:LQ1 + 64] = f("lambda_q1")[L]; prow[0, LK1:LK1 + 64] = f("lambda_k1")[L]
        prow[0, LQ2:LQ2 + 64] = f("lambda_q2")[L]; prow[0, LK2:LK2 + 64] = f("lambda_k2")[L]
        maps.append({
            "x": np.ascontiguousarray(x), "pos": np.ascontiguousarray(pos.reshape(1, T)),
            "w_ada": f("w_ada")[L], "b_ada": f("b_ada")[L].reshape(1, -1),
            "w_in": f("w_in")[L], "w_out": f("w_out")[L], "w_ffn_in": f("w_ffn_in")[L], "w_ffn_out": f("w_ffn_out")[L],
            "cmat": cmat, "pcol": pcol, "prow": prow,
        })
    return maps


def kernel(**inputs):
    if "nc" not in _CACHE:
        _CACHE["nc"] = build()[0]
    nc = _CACHE["nc"]
    maps = make_in_maps(inputs)
    res = run_bass_kernel_spmd(nc, maps, core_ids=list(range(8)))
    out = np.zeros((4, T, D), np.float32)
    for core in range(8):
        b = core // 2; half = core % 2
        o = np.asarray(res.results[core]["out"], np.float32)
        if half == 0:
            out[b, :NOWN] = o
        else:
            out[b, NOWN:] = o[::-1]
    return out
```
